# Optimizing a Trainium2 kernel written in Bass

```python
import jax, jax.numpy as jnp
from jax import lax
import numpy as np

D_MODEL = 2048
BATCH = 4
SEQ = 2048
DEPTH = 1
DEC_BATCH = 128
DEC_SEQ = 1
PAST_LEN = 16384
PAGE_SIZE = 128

D_A = D_MODEL // 2
HEAD = 64
H_A = D_A // HEAD
LORA_W = 64
LORA_A = 64
D_B = D_MODEL // 2
CONV_K = 31
PLE_DIM = 256
SHIFT_W = 3 * D_A + LORA_W + LORA_A
N_IN = SHIFT_W + D_A + 2 * D_B + D_B + 2 * D_MODEL
RMS_EPS = 1e-6
LN_EPS = 1e-5
GN_EPS = 64e-5

kernel_name = 'rwkv7_conformer_gated_hybrid_step'


def _rmsnorm(x, g):
    xf = x.astype(jnp.float32)
    xf = xf * lax.rsqrt(jnp.mean(xf * xf, axis=-1, keepdims=True) + RMS_EPS)
    return (xf * g.astype(jnp.float32)).astype(x.dtype)


def _wkv7_scan(S0, r, w, k, v, a, b):
    def step(S, inp):
        r_t, w_t, k_t, v_t, a_t, b_t = inp
        sa = jnp.einsum('bhij,bhj->bhi', S, a_t)
        S = S * w_t[:, :, None, :] + sa[..., None] * b_t[:, :, None, :] + v_t[..., None] * k_t[:, :, None, :]
        return S, jnp.einsum('bhij,bhj->bhi', S, r_t)
    seq = tuple(jnp.swapaxes(t.astype(jnp.float32), 0, 1) for t in (r, w, k, v, a, b))
    S, ys = lax.scan(step, S0.astype(jnp.float32), seq)
    return S, jnp.swapaxes(ys, 0, 1)


def _layer(x, p, shift_prev, wkv_prev, conv_prev, norm_g, w_in, shift_mu, w0, w_lora_b, a0, a_lora_b,
           k_k, k_a, r_k, lnx_g, lnx_b, w_proj_a, conv_w, conv_b, cln_g, cln_b, w_proj_b, w_out,
           w_ple, w_ple_gate):
    f32 = jnp.float32
    Bn, T, _ = x.shape
    dt = x.dtype
    xn = _rmsnorm(x, norm_g)
    proj = jnp.einsum('btd,dn->btn', xn, w_in)
    o1 = SHIFT_W
    o2 = o1 + D_A
    o3 = o2 + 2 * D_B
    o4 = o3 + D_B
    p_shift = proj[..., :o1]
    gate_a = proj[..., o1:o2]
    glu_in = proj[..., o2:o3]
    gate_b = proj[..., o3:o4]
    merge = proj[..., o4:]

    prev = jnp.concatenate([shift_prev[:, None].astype(dt), p_shift[:, :-1]], axis=1)
    xs = p_shift + shift_mu * (prev - p_shift)
    new_shift = p_shift[:, -1]
    r = xs[..., :D_A]
    k = xs[..., D_A:2 * D_A]
    v = xs[..., 2 * D_A:3 * D_A]
    xw = xs[..., 3 * D_A:3 * D_A + LORA_W]
    xa = xs[..., 3 * D_A + LORA_W:]
    w_log = -jax.nn.softplus(-(w0 + jnp.tanh(xw) @ w_lora_b).astype(f32)) - 0.5
    decay = jnp.exp(-jnp.exp(w_log))
    a = jax.nn.sigmoid((a0 + xa @ a_lora_b).astype(f32))
    hd = lambda t: t.reshape(Bn, T, H_A, HEAD)
    kk = hd(k.astype(f32) * k_k)
    kk = kk / jnp.maximum(jnp.sqrt(jnp.sum(kk * kk, axis=-1, keepdims=True)), 1e-12)
    kf = hd(k.astype(f32) * (1.0 + (a - 1.0) * k_a))
    rf = hd(r.astype(f32))
    vf = hd(v.astype(f32))
    S, o = _wkv7_scan(wkv_prev, rf, hd(decay), kf, vf, -kk, kk * hd(a))
    mu = jnp.mean(o, axis=-1, keepdims=True)
    var = jnp.mean(jnp.square(o - mu), axis=-1, keepdims=True)
    o = ((o - mu) * lax.rsqrt(var + GN_EPS)).reshape(Bn, T, D_A) * lnx_g + lnx_b
    bonus = jnp.sum(rf * kf * r_k, axis=-1, keepdims=True) * vf
    o = o + bonus.reshape(Bn, T, D_A)
    y_a = jnp.einsum('btc,cd->btd', (o * jax.nn.silu(gate_a.astype(f32))).astype(dt), w_proj_a)

    u = glu_in[..., :D_B] * jax.nn.sigmoid(glu_in[..., D_B:])
    ucat = jnp.concatenate([conv_prev.astype(dt), u], axis=1)
    c = lax.conv_general_dilated(ucat, conv_w[:, None, :].astype(dt), (1,), 'VALID',
                                 dimension_numbers=('NWC', 'WIO', 'NWC'),
                                 feature_group_count=D_B) + conv_b
    new_conv = ucat[:, -(CONV_K - 1):]
    cf = c.astype(f32)
    cm = jnp.mean(cf, axis=-1, keepdims=True)
    cv = jnp.mean(jnp.square(cf - cm), axis=-1, keepdims=True)
    cf = (cf - cm) * lax.rsqrt(cv + LN_EPS) * cln_g + cln_b
    cb = jax.nn.silu(cf) * jax.nn.silu(gate_b.astype(f32))
    y_b = jnp.einsum('btc,cd->btd', cb.astype(dt), w_proj_b)

    ga = merge[..., :D_MODEL]
    gb = merge[..., D_MODEL:]
    m = jax.nn.sigmoid(ga) * y_a + jax.nn.sigmoid(gb) * y_b
    h = x + jnp.einsum('btd,de->bte', m, w_out)
    h = h + jax.nn.sigmoid(jnp.einsum('btd,de->bte', h, w_ple_gate)) * jnp.einsum('btp,pd->btd', p, w_ple)
    return h, new_shift, S.astype(wkv_prev.dtype), new_conv


def setup_inputs(seed: int = 0) -> dict:
    key = jax.random.key(seed)
    ks = jax.random.split(key, 32)
    nrm = lambda k, s, sc: jax.random.normal(k, s, jnp.float32) * sc
    L = DEPTH
    return {
        'x_prompt': nrm(ks[0], (BATCH, SEQ, D_MODEL), 1.0),
        'x_sample': nrm(ks[1], (DEC_BATCH, DEC_SEQ, D_MODEL), 1.0),
        'state_shift': nrm(ks[2], (L, DEC_BATCH, SHIFT_W), 1.0),
        'state_wkv': nrm(ks[3], (L, DEC_BATCH, H_A, HEAD, HEAD), 0.3),
        'state_conv': nrm(ks[4], (L, DEC_BATCH, CONV_K - 1, D_B), 0.5),
        'p_prompt': nrm(ks[5], (L, BATCH, SEQ, PLE_DIM), 1.0),
        'p_sample': nrm(ks[6], (L, DEC_BATCH, DEC_SEQ, PLE_DIM), 1.0),
        'norm_g': 1.0 + nrm(ks[7], (L, D_MODEL), 0.01),
        'w_in': nrm(ks[8], (L, D_MODEL, N_IN), D_MODEL ** -0.5),
        'shift_mu': jax.random.uniform(ks[9], (L, SHIFT_W), jnp.float32),
        'w0': jax.random.uniform(ks[10], (L, D_A), jnp.float32, -6.0, 0.0),
        'w_lora_b': nrm(ks[11], (L, LORA_W, D_A), 0.1 * LORA_W ** -0.5),
        'a0': nrm(ks[12], (L, D_A), 0.1),
        'a_lora_b': nrm(ks[13], (L, LORA_A, D_A), 0.1 * LORA_A ** -0.5),
        'k_k': 0.85 + nrm(ks[14], (L, D_A), 0.02),
        'k_a': 1.0 + nrm(ks[15], (L, D_A), 0.02),
        'r_k': nrm(ks[16], (L, H_A, HEAD), 0.1),
        'lnx_g': 1.0 + nrm(ks[17], (L, D_A), 0.01),
        'lnx_b': nrm(ks[18], (L, D_A), 0.01),
        'w_proj_a': nrm(ks[19], (L, D_A, D_MODEL), D_A ** -0.5),
        'conv_w': nrm(ks[20], (L, CONV_K, D_B), CONV_K ** -0.5),
        'conv_b': nrm(ks[21], (L, D_B), 0.01),
        'cln_g': 1.0 + nrm(ks[22], (L, D_B), 0.01),
        'cln_b': nrm(ks[23], (L, D_B), 0.01),
        'w_proj_b': nrm(ks[24], (L, D_B, D_MODEL), D_B ** -0.5),
        'w_out': nrm(ks[25], (L, D_MODEL, D_MODEL), D_MODEL ** -0.5),
        'w_ple': nrm(ks[26], (L, PLE_DIM, D_MODEL), PLE_DIM ** -0.5),
        'w_ple_gate': nrm(ks[27], (L, D_MODEL, D_MODEL), D_MODEL ** -0.5),
        'final_g': 1.0 + nrm(ks[28], (D_MODEL,), 0.01),
    }


def reference(x_prompt, x_sample, state_shift, state_wkv, state_conv, p_prompt, p_sample,
              norm_g, w_in, shift_mu, w0, w_lora_b, a0, a_lora_b, k_k, k_a, r_k, lnx_g, lnx_b,
              w_proj_a, conv_w, conv_b, cln_g, cln_b, w_proj_b, w_out, w_ple, w_ple_gate, final_g):
    Bp = x_prompt.shape[0]
    hp, hs = x_prompt, x_sample
    ps_shift, ps_wkv, ps_conv = [], [], []
    ss_shift, ss_wkv, ss_conv = [], [], []
    for i in range(DEPTH):
        lw = (norm_g[i], w_in[i], shift_mu[i], w0[i], w_lora_b[i], a0[i], a_lora_b[i], k_k[i], k_a[i],
              r_k[i], lnx_g[i], lnx_b[i], w_proj_a[i], conv_w[i], conv_b[i], cln_g[i], cln_b[i],
              w_proj_b[i], w_out[i], w_ple[i], w_ple_gate[i])
        hp, sh, wk, cv = _layer(hp, p_prompt[i],
                                jnp.zeros((Bp, SHIFT_W), x_prompt.dtype),
                                jnp.zeros((Bp, H_A, HEAD, HEAD), jnp.float32),
                                jnp.zeros((Bp, CONV_K - 1, D_B), x_prompt.dtype), *lw)
        ps_shift.append(sh)
        ps_wkv.append(wk)
        ps_conv.append(cv)
        hs, sh, wk, cv = _layer(hs, p_sample[i], state_shift[i], state_wkv[i], state_conv[i], *lw)
        ss_shift.append(sh)
        ss_wkv.append(wk)
        ss_conv.append(cv)
    y_prompt = _rmsnorm(hp, final_g)
    y_sample = _rmsnorm(hs, final_g)
    return (y_prompt, y_sample, jnp.stack(ps_shift), jnp.stack(ps_wkv), jnp.stack(ps_conv),
            jnp.stack(ss_shift), jnp.stack(ss_wkv), jnp.stack(ss_conv))
```

```python
import os
import numpy as np
import ml_dtypes
import concourse.bass as bass
import concourse.mybir as mybir
from concourse.bass_utils import run_bass_kernel_spmd

F32 = mybir.dt.float32
BF16 = mybir.dt.bfloat16
AF = mybir.ActivationFunctionType
ALU = mybir.AluOpType
AX = mybir.AxisListType

D = 2048
DA = 1024
SHIFT_W = 3200
O1 = 3200
O2 = O1 + 1024
O3 = O2 + 2048
O4 = O3 + 1024
N_IN = O4 + 4096
NCORE = 8
NS = 16
TOWN = 1024
C0 = float(np.exp(-0.5))

CV_MU = 0
CV_W0 = 25
CV_A0 = 33
CV_KK = 41
CV_KA = 49
CV_RK = 57
CV_LG = 65
CV_LB = 73
CV_CB = 81
CV_CG = 89
CV_CLB = 97
CV_CW = 105
CV_N = 105 + 248
MK_SC = 0
MK_MU = 128
MK_ML = 192
MK_RM = 256
MK_BO = 768
MK_ON = 896
MK_N = 1024


class Reg:
    __slots__ = ("name", "w", "rs")

    def __init__(self, name):
        self.name = name
        self.w = None
        self.rs = []


class Sched:
    def __init__(self, nc, n_dma_sems=6):
        self.nc = nc
        self.engs = {}
        self.ops = {}
        self.sems = {}
        self.known = {}
        self.count = {}
        for name in ("tensor", "vector", "scalar", "gpsimd", "sync"):
            self.engs[name] = getattr(nc, name)
            self.ops[name] = []
            self.known[name] = {}
        for name in ("tensor", "vector", "scalar", "gpsimd"):
            self._mksem("e_" + name)
        self.dma_sems = {}
        self.dma_rr = {}
        for q in ("sync", "gpsimd", "scalar"):
            self.dma_sems[q] = []
            for i in range(n_dma_sems):
                k = "d_%s%d" % (q, i)
                self._mksem(k)
                self.dma_sems[q].append(k)
            self.dma_rr[q] = 0
        self.nwaits = 0
        self.nops = 0
        self.epoch = {}

    def _mksem(self, key):
        self.sems[key] = self.nc.alloc_semaphore(key)
        self.count[key] = 0

    def _deps(self, reads, writes):
        deps = {}
        for r in reads:
            if r.w is not None:
                k, v = r.w
                deps[k] = max(deps.get(k, 0), v)
        for r in writes:
            if r.w is not None:
                k, v = r.w
                deps[k] = max(deps.get(k, 0), v)
            for (k, v) in r.rs:
                deps[k] = max(deps.get(k, 0), v)
        return deps

    def _emit_waits(self, eng, deps):
        kn = self.known[eng]
        e = self.engs[eng]
        for k, v in deps.items():
            if kn.get(k, 0) >= v:
                continue
            kn[k] = v
            sem = self.sems[k]
            self.ops[eng].append(lambda e=e, sem=sem, v=v: e.wait_ge(sem, v))
            self.nwaits += 1

    def _record(self, ev, reads, writes):
        for r in reads:
            r.rs.append(ev)
            if len(r.rs) > 64:
                m = {}
                for (k, v) in r.rs:
                    m[k] = max(m.get(k, 0), v)
                r.rs = list(m.items())
        for r in writes:
            r.w = ev
            r.rs = []

    SEM_LIMIT = 3500

    def _cur_key(self, basekey):
        ep = self.epoch.get(basekey, 0)
        key = basekey if ep == 0 else "%s#%d" % (basekey, ep)
        if self.count[key] >= self.SEM_LIMIT:
            ep += 1
            self.epoch[basekey] = ep
            key = "%s#%d" % (basekey, ep)
            self._mksem(key)
        return key

    def op(self, eng, fn, reads=(), writes=(), sync_same=True):
        key = self._cur_key("e_" + eng)
        deps = self._deps(reads, writes)
        if not sync_same:
            for k in [k for k in deps if k.split("#")[0] == "e_" + eng]:
                deps.pop(k, None)
        self._emit_waits(eng, deps)
        self.count[key] += 1
        v = self.count[key]
        sem = self.sems[key]
        e = self.engs[eng]
        self.ops[eng].append(lambda: fn(e).then_inc(sem, 1))
        self.nops += 1
        ev = (key, v)
        self._record(ev, reads, writes)
        return ev

    def op_group(self, eng, items):
        base = "e_" + eng
        deps = {}
        for (fn, reads, writes) in items:
            for k, v in self._deps(reads, writes).items():
                deps[k] = max(deps.get(k, 0), v)
        for k in [k for k in deps if k.split("#")[0] == base]:
            deps.pop(k, None)
        self._emit_waits(eng, deps)
        for (fn, reads, writes) in items:
            self.op(eng, fn, reads, writes, sync_same=False)

    def dma(self, q, out, in_, reads=(), writes=(), **kw):
        i = self.dma_rr[q]
        self.dma_rr[q] = (i + 1) % len(self.dma_sems[q])
        key = self._cur_key(self.dma_sems[q][i])
        deps = self._deps(reads, writes)
        if self.count[key] > 0:
            deps[key] = max(deps.get(key, 0), self.count[key])
        self._emit_waits(q, deps)
        self.count[key] += 16
        v = self.count[key]
        sem = self.sems[key]
        e = self.engs[q]
        self.ops[q].append(lambda: e.dma_start(out=out, in_=in_, **kw).then_inc(sem, 16))
        self.nops += 1
        ev = (key, v)
        self._record(ev, reads, writes)
        return ev

    def barrier(self):
        for eng in ("tensor", "vector", "scalar", "gpsimd", "sync"):
            deps = {k: v for k, v in self.count.items() if v > 0}
            self._emit_waits(eng, deps)

    def emit(self):
        nc = self.nc
        with nc.Block() as block:
            @block.sync
            def _(e):
                for f in self.ops["sync"]:
                    f()

            @block.tensor
            def _(e):
                for f in self.ops["tensor"]:
                    f()

            @block.vector
            def _(e):
                for f in self.ops["vector"]:
                    f()

            @block.scalar
            def _(e):
                for f in self.ops["scalar"]:
                    f()

            @block.gpsimd
            def _(e):
                for f in self.ops["gpsimd"]:
                    f()


def build(stage=99):
    POOL = os.environ.get('KPOOL', 'vector')
    SUB = int(os.environ.get('KSUB', '9'))
    nc = bass.Bass("TRN2", target_bir_lowering=False)
    S = Sched(nc)

    def din(name, shape, dt=F32):
        return nc.dram_tensor(name, list(shape), dt, kind="ExternalInput").ap()

    def dout(name, shape, dt=F32):
        return nc.dram_tensor(name, list(shape), dt, kind="ExternalOutput").ap()

    xs_d = din("xs", [2048 + NS, D])
    w_in_d = din("w_in", [D, N_IN])
    w_pa_d = din("w_pa", [DA, D])
    w_pb_d = din("w_pb", [DA, D])
    w_out_d = din("w_out", [D, D])
    w_pg_d = din("w_pg", [D, D])
    w_ple_d = din("w_ple", [256, D])
    ng_d = din("ng", [1, D])
    fg_d = din("fg", [1, D])
    fgT_d = din("fgT", [128, 16])
    cvec_d = din("cvec", [128, CV_N])
    lwp_d = din("lwp", [128, 2, DA])
    ssT_d = din("ssT", [128, 25, NS])
    wkvT_d = din("wkvT", [NS, 8, 128, 64])
    scT_d = din("scT", [128, 8, NS, 30])
    pp_d = din("pp", [TOWN + NS, 256])
    identb_d = din("identb", [128, 128], BF16)
    identf_d = din("identf", [128, 128])
    msk_d = din("msk", [128, MK_N])

    y_d = dout("y", [TOWN + NS, D])
    nsp_d = dout("nsp", [128, 25])
    nwp_d = dout("nwp", [128, 8, 64])
    ncp_d = dout("ncp", [128, 8, 30])
    nss_d = dout("nss", [128, 25, NS])
    nws_d = dout("nws", [NS, 8, 128, 64])
    ncs_d = dout("ncs", [128, 8, NS, 30])
    out_regs = {k: Reg("o_" + k) for k in ["y", "nsp", "nwp", "ncp", "nss", "nws", "ncs"]}

    ARENA = 205440
    slab = nc.alloc_sbuf_tensor("slab", [128, ARENA // 4], F32)
    base = nc.lookup_mloc(slab).addr
    cur = {"P": 0, "U": 0}
    PSZ = 102 * 1024
    names = [0]

    def sb(name, shape, dt=F32, reg="P"):
        nbytes = int(np.prod(shape[1:])) * (2 if dt == BF16 else 4)
        nbytes = (nbytes + 63) // 64 * 64
        off = cur[reg]
        cur[reg] += nbytes
        if reg == "P":
            assert cur[reg] <= PSZ, (name, cur[reg])
            a = base + off
        else:
            assert PSZ + cur[reg] <= ARENA, (name, cur[reg])
            a = base + PSZ + off
        names[0] += 1
        return nc.alloc_sbuf_tensor_at("t%d_%s" % (names[0], name), list(shape), dt, offset=a)

    def reset_union():
        S.barrier()
        cur["U"] = 0

    cvec = sb("cvec", [128, CV_N]); r_cvec = Reg("cvec")
    identb = sb("identb", [128, 128], BF16); r_identb = Reg("identb")
    identf = sb("identf", [128, 128]); r_identf = Reg("identf")
    msk = sb("msk", [128, MK_N]); r_msk = Reg("msk")
    lwp = sb("lwp", [128, 2, DA]); r_lwp = Reg("lwp")
    omka = sb("omka", [128, 8]); r_omka = Reg("omka")
    S.dma("sync", cvec[:], cvec_d, writes=[r_cvec])
    S.dma("sync", identb[:], identb_d, writes=[r_identb])
    S.dma("sync", identf[:], identf_d, writes=[r_identf])
    S.dma("sync", msk[:], msk_d, writes=[r_msk])
    S.dma("sync", lwp[:], lwp_d, writes=[r_lwp])
    S.op("vector", lambda e: e.tensor_scalar(out=omka[:], in0=cvec[:, CV_KA:CV_KA + 8], scalar1=-1.0, scalar2=1.0,
                                             op0=ALU.mult, op1=ALU.add), reads=[r_cvec], writes=[r_omka])
    MSC = msk[:, MK_SC:MK_SC + 128]
    MU = msk[0:64, MK_MU:MK_MU + 64]
    ML = msk[0:64, MK_ML:MK_ML + 64]
    RMASK = msk[:, MK_RM:MK_RM + 512]
    BONES = msk[:, MK_BO:MK_BO + 128]
    ONES = msk[:, MK_ON:MK_ON + 128]
    IDF64 = identf[0:64, 0:64]

    pg = [nc.alloc_psum_tensor("pg%d" % i, [128, 512], F32) for i in range(2)]
    r_pg = [Reg("pg%d" % i) for i in range(2)]
    pgs = nc.alloc_psum_tensor("pgs", [128, 512], F32); r_pgs = Reg("pgs")
    ptb = nc.alloc_psum_tensor("ptb", [128, 1024], BF16); r_ptb = Reg("ptb")
    pq = [nc.alloc_psum_tensor("pq%d" % i, [128, 512], F32) for i in range(4)]
    r_pq = [Reg("pq%d" % i) for i in range(4)]

    XW = TOWN + NS
    xnT = sb("xnT", [128, 16, XW], BF16); r_xnT = Reg("xnT")
    xhist = sb("xhist", [128, 16, 32], BF16); r_xhist = Reg("xhist")
    NSLOT = 6
    wF = sb("wF", [128, NSLOT, 16, 128], BF16)
    r_wF = [Reg("wF%d" % i) for i in range(NSLOT)]
    ogT = sb("ogT", [128, 8, XW], BF16); r_og = [Reg("og%d" % g) for g in range(8)]
    carry = sb("carry", [128, 25]); r_carry = Reg("carry")
    ssT = sb("ssT", [128, 25, NS]); r_ssT = Reg("ssT")
    nss_sb = sb("nss_sb", [128, 25, NS]); r_nss = Reg("nss_sb")
    Hsave = sb("Hsave", [128, 8, 128]); r_Hsave = [Reg("Hsave%d" % g) for g in range(8)]
    nwp_sb = sb("nwp_sb", [128, 8, 64]); r_nwp = Reg("nwp_sb")
    ssq = sb("ssq", [128, 2]); r_ssq = [Reg("ssq0"), Reg("ssq1")]
    S.dma("sync", ssT[:], ssT_d, writes=[r_ssT])
    S.op(POOL, lambda e: e.memset(carry[:], 0.0), writes=[r_carry])
    S.op(POOL, lambda e: e.memset(Hsave[:], 0.0), writes=r_Hsave)

    chunks = []

    def add_chunk(src, col, K=D):
        chunks.append((src[:, col:col + 128], K // 128))
        return len(chunks) - 1

    issued = [0]
    released = set()
    low = [0]

    def wpump():
        while low[0] in released:
            low[0] += 1
        while issued[0] < len(chunks) and issued[0] < low[0] + NSLOT:
            j = issued[0]
            src, kc = chunks[j]
            sl = j % NSLOT
            if not (os.environ.get("KNOW") == "1" and j >= NSLOT):
                S.dma("gpsimd", wF[:, sl, 0:kc, :], src.rearrange("(c p) n -> p c n", p=128), writes=[r_wF[sl]])
            issued[0] += 1

    def wdone(ci):
        released.add(ci)
        wpump()

    def wget(ci):
        wpump()
        assert issued[0] > ci, (ci, issued[0], low[0])
        return ci % NSLOT, chunks[ci][1]

    def gemmF(ci, rhs_t, r_rhs, toks):
        sl, kc = wget(ci)
        for (c0, n, pap, preg) in toks:
            for k in range(kc):
                S.op("tensor", lambda e, sl=sl, k=k, c0=c0, n=n, pap=pap, kc=kc: e.matmul(
                    pap, lhsT=wF[:, sl, k, :], rhs=rhs_t[:, k, c0:c0 + n], start=(k == 0), stop=(k == kc - 1)),
                     reads=[r_wF[sl], r_rhs], writes=[preg], sync_same=False)

    plan = {}
    for ps in ("P", "O"):
        plan[(ps, "xwxa")] = add_chunk(w_in_d, 3072)
        for g in range(8):
            for nm, b0 in (("r", 0), ("k", 1024), ("v", 2048)):
                plan[(ps, nm, g)] = add_chunk(w_in_d, b0 + 128 * g)
            if ps == "O":
                plan[("ga", g)] = add_chunk(w_in_d, O1 + 128 * g)
    for c in range(8):
        plan[("glua", c)] = add_chunk(w_in_d, O2 + 128 * c)
        plan[("glub", c)] = add_chunk(w_in_d, O2 + 1024 + 128 * c)
    for c in range(8):
        plan[("gb", c)] = add_chunk(w_in_d, O3 + 128 * c)
    for dc in range(16):
        plan[("pa", dc)] = add_chunk(w_pa_d, 128 * dc, K=DA)
        plan[("mga", dc)] = add_chunk(w_in_d, O4 + 128 * dc)
        plan[("pb", dc)] = add_chunk(w_pb_d, 128 * dc, K=DA)
        plan[("mgb", dc)] = add_chunk(w_in_d, O4 + 2048 + 128 * dc)
    for hh in range(2):
        for ec in range(16):
            plan[("wo", ec, hh)] = add_chunk(w_out_d, 128 * ec)
        for fc in range(16):
            plan[("pgt", fc, hh)] = add_chunk(w_pg_d, 128 * fc)
            plan[("ple", fc, hh)] = add_chunk(w_ple_d, 128 * fc, K=256)

    globals_cache = {}

    def run_pass(ps):
        has_s = (ps == "O")
        reset_union()
        gbc = sb("gbc", [128, D], reg="U"); r_gbc = Reg("gbc")
        xt = [sb("xt%d" % i, [128, D], reg="U") for i in range(2)]
        r_xt = [Reg("xt%d" % i) for i in range(2)]
        xnb = [sb("xnb%d" % i, [128, D], BF16, reg="U") for i in range(2)]
        r_xnb = [Reg("xnb%d" % i) for i in range(2)]
        junk = sb("junk", [128, D], BF16, reg="U"); r_junk = Reg("junk")
        S.dma("sync", gbc[:], ng_d.partition_broadcast(128), writes=[r_gbc])
        tilecnt = [0]

        def build_xnT(row0, nrows, col0):
            t0 = 0
            while t0 < nrows:
                n = min(128, nrows - t0)
                i = tilecnt[0] % 2
                tilecnt[0] += 1
                S.dma("sync", xt[i][0:n, :], xs_d[row0 + t0:row0 + t0 + n, :], writes=[r_xt[i]])
                S.op("scalar", lambda e, i=i, n=n: e.activation(out=junk[0:n, :], in_=xt[i][0:n, :], func=AF.Square,
                                                                accum_out=ssq[0:n, i:i + 1]),
                     reads=[r_xt[i]], writes=[r_junk, r_ssq[i]])
                S.op("scalar", lambda e, i=i, n=n: e.activation(out=ssq[0:n, i:i + 1], in_=ssq[0:n, i:i + 1], func=AF.Sqrt,
                                                                scale=1.0 / D, bias=1e-6),
                     reads=[r_ssq[i]], writes=[r_ssq[i]])
                S.op("vector", lambda e, i=i, n=n: e.reciprocal(out=ssq[0:n, i:i + 1], in_=ssq[0:n, i:i + 1]),
                     reads=[r_ssq[i]], writes=[r_ssq[i]])
                S.op("vector", lambda e, i=i, n=n: e.scalar_tensor_tensor(out=xnb[i][0:n, :], in0=xt[i][0:n, :],
                                                                          scalar=ssq[0:n, i:i + 1], in1=gbc[0:n, :],
                                                                          op0=ALU.mult, op1=ALU.mult),
                     reads=[r_xt[i], r_ssq[i], r_gbc], writes=[r_xnb[i]])
                for half in range(2):
                    for c in range(8):
                        cc = half * 8 + c
                        S.op("tensor", lambda e, i=i, n=n, c=c, cc=cc: e.transpose(
                            ptb[:, c * 128:c * 128 + n], xnb[i][0:n, cc * 128:(cc + 1) * 128], identb[0:n, 0:n]),
                             reads=[r_xnb[i], r_identb], writes=[r_ptb], sync_same=False)
                    S.op("scalar", lambda e, half=half, n=n, t0=t0: e.copy(
                        out=xnT[:, half * 8:(half + 1) * 8, col0 + t0:col0 + t0 + n],
                        in_=ptb[:].rearrange("p (c t) -> p c t", c=8)[:, :, 0:n]),
                         reads=[r_ptb], writes=[r_xnT])
                t0 += n

        if ps == "P":
            build_xnT(0, 1024, 0)
            S.op("vector", lambda e: e.tensor_copy(out=xhist[:], in_=xnT[:, :, 992:1024]), reads=[r_xnT], writes=[r_xhist])
        else:
            build_xnT(1024, 1024 + NS, 0)
        reset_union()

        def ub(name, shape, dt=F32):
            return sb(name, shape, dt, reg="U"), Reg(name)

        praw = {}
        r_praw = {}
        _pr, _rpr = ub("praw", [128, 513])
        for nm in ("r", "k", "v", "x"):
            praw[nm], r_praw[nm] = _pr, _rpr
        Hin, r_Hin = ub("Hin", [128, 8, 64])
        Hout, r_Hout = ub("Hout", [128, 8, 64])
        tmpd, r_tmpd = ub("tmpd", [128, 512])
        stmp, r_stmp = ub("stmp", [128, NS])
        XR, r_XR = ub("XR", [128, 512]); XK, r_XK = ub("XK", [128, 512]); XV, r_XV = ub("XV", [128, 512])
        XRs, r_XRs = ub("XRs", [128, NS]); XKs, r_XKs = ub("XKs", [128, NS]); XVs, r_XVs = ub("XVs", [128, NS])
        LWIN, r_LWIN = ub("LWIN", [128, TOWN + NS])
        LD, r_LD = ub("LD", [128, 512]); AA, r_AA = ub("AA", [128, 512])
        LDs, r_LDs = ub("LDs", [128, NS]); AAs, r_AAs = ub("AAs", [128, NS])
        KK, r_KK = ub("KK", [128, 512]); T1, r_T1 = ub("T1", [128, 512]); T2, r_T2 = ub("T2", [128, 512])
        KF, r_KF = ub("KF", [128, 512]); BV, r_BV = ub("BV", [128, 512]); BON, r_BON = ub("BON", [128, 512])
        CUM, r_CUM = ub("CUM", [128, 512]); EP, r_EP = ub("EP", [128, 512]); EM, r_EM = ub("EM", [128, 512])
        AR, r_AR = ub("AR", [128, 8, 2, 64])
        BKp = []; r_BKp = []
        SC = []; r_SC = []
        BKtp = []; r_BKtp = []
        UVp = []; r_UVp = []
        for hd in range(2):
            t, r = ub("BKp%d" % hd, [128, 8, 2, 64]); BKp.append(t); r_BKp.append(r)
            t, r = ub("SC%d" % hd, [128, 8, 2, 64]); SC.append(t); r_SC.append(r)
            t, r = ub("BKtp%d" % hd, [128, 8, 128]); BKtp.append(t); r_BKtp.append(r)
            t, r = ub("UVp%d" % hd, [128, 8, 128]); UVp.append(t); r_UVp.append(r)
            S.op(POOL, lambda e, t=BKp[hd]: e.memset(t[:], 0.0), writes=[r_BKp[hd]])
            S.op(POOL, lambda e, t=UVp[hd]: e.memset(t[:], 0.0), writes=[r_UVp[hd]])
        Ttp, r_Ttp = ub("Ttp", [128, 8, 2, 128])
        S.op(POOL, lambda e: e.memset(Ttp[:], 0.0), writes=[r_Ttp])
        GC, r_GC = ub("GC", [128, 8])
        PSb, r_PSb = ub("PSb", [128, 8, 2, 64], BF16)
        Qb, r_Qb = ub("Qb", [128, 8, 64], BF16)
        S.op(POOL, lambda e: e.memset(PSb[:], 0.0), writes=[r_PSb])
        S.op(POOL, lambda e: e.memset(Qb[:], 0.0), writes=[r_Qb])
        SfA, r_SfA = EP[0:64, :].rearrange("p (c t) -> p c t", t=64), r_EP
        SfB, r_SfB = CUM[0:64, :].rearrange("p (c t) -> p c t", t=64), r_CUM
        Pm, r_Pm = EM[0:64, :].rearrange("p (c t) -> p c t", t=64), r_EM
        Hbd = []; r_Hbd = []
        for i in range(2):
            if ("Hbd%d" % i) not in globals_cache:
                globals_cache["Hbd%d" % i] = sb("Hbdp%d" % i, [128, 128])
            t, r = globals_cache["Hbd%d" % i], Reg("Hbd%d" % i)
            Hbd.append(t); r_Hbd.append(r)
            S.op(POOL, lambda e, t=t: e.memset(t[:], 0.0), writes=[r])
        if "HGp" not in globals_cache:
            globals_cache["HGp"] = sb("HGp", [128, 128])
        HG, r_HG = globals_cache["HGp"], Reg("HG")
        Zs, r_Zs = ub("Zs", [128, 128])
        S.op(POOL, lambda e: e.memset(Zs[:], 0.0), writes=[r_Zs])
        YT, r_YT = BV, r_BV
        YTs, r_YTs = ub("YTs", [128, NS]); BONs, r_BONs = ub("BONs", [128, NS])
        Up = []; r_Up = []
        for hd in range(2):
            t, r = ub("Up%d" % hd, [128, 8, 128]); Up.append(t); r_Up.append(r)
            S.op(POOL, lambda e, t=t: e.memset(t[:], 0.0), writes=[r])
        G1, r_G1 = KK, r_KK
        G2, r_G2 = T1, r_T1
        G3, r_G3 = T2, r_T2

        def shift_evac(nm, ci25, pap, preg, n, dst, r_dst):
            pr = praw[nm]
            rp = r_praw[nm]
            S.op("scalar", lambda e: e.copy(out=pr[:, 0:1], in_=carry[:, ci25:ci25 + 1]), reads=[r_carry], writes=[rp])
            S.op("scalar", lambda e: e.copy(out=pr[:, 1:1 + n], in_=pap), reads=[preg], writes=[rp])
            S.op("scalar", lambda e: e.copy(out=carry[:, ci25:ci25 + 1], in_=pr[:, n:n + 1]), reads=[rp], writes=[r_carry])
            S.op("vector", lambda e: e.tensor_sub(out=tmpd[:, 0:n], in0=pr[:, 0:n], in1=pr[:, 1:1 + n]),
                 reads=[rp], writes=[r_tmpd])
            S.op("vector", lambda e: e.scalar_tensor_tensor(out=dst, in0=tmpd[:, 0:n], scalar=cvec[:, CV_MU + ci25:CV_MU + ci25 + 1],
                                                            in1=pr[:, 1:1 + n], op0=ALU.mult, op1=ALU.add),
                 reads=[r_tmpd, rp, r_cvec], writes=[r_dst])

        def shift_evac_sample(ci25, pap, preg, dst, r_dst):
            S.op("scalar", lambda e: e.copy(out=nss_sb[:, ci25, :], in_=pap), reads=[preg], writes=[r_nss])
            S.op("vector", lambda e: e.tensor_sub(out=stmp[:], in0=ssT[:, ci25, :], in1=nss_sb[:, ci25, :]),
                 reads=[r_ssT, r_nss], writes=[r_stmp])
            S.op("vector", lambda e: e.scalar_tensor_tensor(out=dst, in0=stmp[:], scalar=cvec[:, CV_MU + ci25:CV_MU + ci25 + 1],
                                                            in1=nss_sb[:, ci25, :], op0=ALU.mult, op1=ALU.add),
                 reads=[r_stmp, r_nss, r_cvec], writes=[r_dst])

        def fl(ap):
            return ap.rearrange("p w t -> p (w t)")

        def c3(t):
            return t[:].rearrange("p (c t) -> p c t", t=64)

        def prep(g, lw_cols, sample_tile, want_y):
            cw0 = cvec[:, CV_W0 + g:CV_W0 + g + 1]
            ca0 = cvec[:, CV_A0 + g:CV_A0 + g + 1]
            ckk = cvec[:, CV_KK + g:CV_KK + g + 1]
            cka = cvec[:, CV_KA + g:CV_KA + g + 1]
            crk = cvec[:, CV_RK + g:CV_RK + g + 1]
            if not sample_tile:
                S.op("tensor", lambda e: e.matmul(pq[0][:, :], lhsT=lwp[:, 0, g * 128:(g + 1) * 128], rhs=LWIN[:, lw_cols:lw_cols + 512],
                                                  start=True, stop=True), reads=[r_lwp, r_LWIN], writes=[r_pq[0]], sync_same=False)
                S.op("tensor", lambda e: e.matmul(pq[1][:, :], lhsT=lwp[:, 1, g * 128:(g + 1) * 128], rhs=LWIN[:, lw_cols:lw_cols + 512],
                                                  start=True, stop=True), reads=[r_lwp, r_LWIN], writes=[r_pq[1]], sync_same=False)
                S.op("scalar", lambda e: e.activation(out=LD[:], in_=pq[0][:, :], func=AF.Sigmoid, bias=cw0),
                     reads=[r_pq[0], r_cvec], writes=[r_LD])
                S.op("scalar", lambda e: e.activation(out=AA[:], in_=pq[1][:, :], func=AF.Sigmoid, bias=ca0),
                     reads=[r_pq[1], r_cvec], writes=[r_AA])
            S.op(POOL, lambda e: e.tensor_scalar(out=KK[:], in0=XK[:], scalar1=ckk, scalar2=None, op0=ALU.mult),
                 reads=[r_XK, r_cvec], writes=[r_KK])
            S.op(POOL, lambda e: e.tensor_mul(out=T1[:], in0=KK[:], in1=KK[:]), reads=[r_KK], writes=[r_T1])
            S.op("tensor", lambda e: e.matmul(pq[2][:, :], lhsT=BONES, rhs=T1[:], start=True, stop=True),
                 reads=[r_msk, r_T1], writes=[r_pq[2]], sync_same=False)
            S.op("scalar", lambda e: e.activation(out=T2[:], in_=pq[2][:, :], func=AF.Sqrt), reads=[r_pq[2]], writes=[r_T2])
            S.op("vector", lambda e: e.tensor_scalar_max(out=T2[:], in0=T2[:], scalar1=1e-12), reads=[r_T2], writes=[r_T2])
            S.op("vector", lambda e: e.reciprocal(out=T2[:], in_=T2[:]), reads=[r_T2], writes=[r_T2])
            S.op("vector", lambda e: e.tensor_mul(out=KK[:], in0=KK[:], in1=T2[:]), reads=[r_KK, r_T2], writes=[r_KK])
            S.op("vector", lambda e: e.tensor_scalar(out=T1[:], in0=AA[:], scalar1=cka, scalar2=omka[:, g:g + 1],
                                                     op0=ALU.mult, op1=ALU.add), reads=[r_AA, r_cvec, r_omka], writes=[r_T1])
            S.op("vector", lambda e: e.tensor_mul(out=KF[:], in0=XK[:], in1=T1[:]), reads=[r_XK, r_T1], writes=[r_KF])
            S.op(POOL, lambda e: e.tensor_mul(out=BV[:], in0=KK[:], in1=AA[:]), reads=[r_KK, r_AA], writes=[r_BV])
            if want_y:
                S.op("vector", lambda e: e.scalar_tensor_tensor(out=T2[:], in0=XR[:], scalar=crk, in1=KF[:],
                                                                op0=ALU.mult, op1=ALU.mult),
                     reads=[r_XR, r_KF, r_cvec], writes=[r_T2])
                S.op("tensor", lambda e: e.matmul(pq[3][:, :], lhsT=BONES, rhs=T2[:], start=True, stop=True),
                     reads=[r_msk, r_T2], writes=[r_pq[3]], sync_same=False)
                S.op("vector", lambda e: e.tensor_mul(out=BON[:], in0=pq[3][:, :], in1=XV[:]),
                     reads=[r_pq[3], r_XV], writes=[r_BON])
            S.op("vector", lambda e: e.tensor_tensor_scan(out=CUM[:], data0=RMASK, data1=LD[:], initial=0.0,
                                                          op0=ALU.mult, op1=ALU.add),
                 reads=[r_msk, r_LD], writes=[r_CUM])
            S.op("scalar", lambda e: e.activation(out=EP[:], in_=CUM[:], func=AF.Exp, scale=-C0), reads=[r_CUM], writes=[r_EP])
            S.op("scalar", lambda e: e.activation(out=EM[:], in_=CUM[:], func=AF.Exp, scale=C0), reads=[r_CUM], writes=[r_EM])
            S.op("vector", lambda e: e.tensor_sub(out=T1[:], in0=CUM[:], in1=LD[:]), reads=[r_CUM, r_LD], writes=[r_T1])
            S.op("scalar", lambda e: e.activation(out=T1[:], in_=T1[:], func=AF.Exp, scale=-C0), reads=[r_T1], writes=[r_T1])
            S.op("vector", lambda e: e.scalar_tensor_tensor(out=AR[:, :, 0, :], in0=c3(KK), scalar=-1.0, in1=c3(T1),
                                                            op0=ALU.mult, op1=ALU.mult),
                 reads=[r_KK, r_T1], writes=[r_AR])
            S.op(POOL, lambda e: e.tensor_mul(out=AR[:, :, 1, :], in0=c3(XR), in1=c3(EP)), reads=[r_XR, r_EP], writes=[r_AR])
            for hd in range(2):
                rs = slice(hd * 64, (hd + 1) * 64)
                S.op("vector", lambda e, hd=hd, rs=rs: e.tensor_mul(out=BKp[hd][rs, :, 0, :], in0=c3(KF)[rs], in1=c3(EM)[rs]),
                     reads=[r_KF, r_EM], writes=[r_BKp[hd]])
                S.op(POOL, lambda e, hd=hd, rs=rs: e.tensor_mul(out=BKp[hd][rs, :, 1, :], in0=c3(BV)[rs], in1=c3(EM)[rs]),
                     reads=[r_BV, r_EM], writes=[r_BKp[hd]])
            S.op("scalar", lambda e: e.copy(out=GC[:], in_=c3(EP)[:, :, 63]), reads=[r_EP], writes=[r_GC])
            for c in range(8):
                b = pq[c // 4]
                S.op("tensor", lambda e, c=c, b=b: e.transpose(b[0:64, (c % 4) * 128:(c % 4 + 1) * 128], XV[:, c * 64:(c + 1) * 64], identf[:, :]),
                     reads=[r_XV, r_identf], writes=[r_pq[c // 4]], sync_same=False)
            for half in range(2):
                src = pq[half][0:64, :].rearrange("p (c x) -> p c x", x=128)
                S.op("scalar", lambda e, half=half, src=src: e.copy(out=UVp[0][0:64, half * 4:(half + 1) * 4, 0:64], in_=src[:, :, 0:64]),
                     reads=[r_pq[half]], writes=[r_UVp[0]])
                S.op("scalar", lambda e, half=half, src=src: e.copy(out=UVp[1][0:64, half * 4:(half + 1) * 4, 64:128], in_=src[:, :, 64:128]),
                     reads=[r_pq[half]], writes=[r_UVp[1]])
            for hd in range(2):
                for c in range(8):
                    b = 2 + (c // 4) % 2
                    S.op("tensor", lambda e, hd=hd, c=c, b=b: e.matmul(pq[b][:, (c % 4) * 128:(c % 4 + 1) * 128],
                                                                       lhsT=fl(BKp[hd][:, c, :, :]), rhs=fl(AR[:, c, :, :]), start=True, stop=True),
                         reads=[r_BKp[hd], r_AR], writes=[r_pq[b]], sync_same=False)
                    if c % 4 == 3:
                        h4 = c // 4
                        S.op("vector", lambda e, hd=hd, h4=h4, b=b: e.tensor_mul(
                            out=SC[hd][:, h4 * 4:(h4 + 1) * 4, :, :].rearrange("p c w t -> p c (w t)"),
                            in0=pq[b][:, :].rearrange("p (c x) -> p c x", x=128),
                            in1=MSC.unsqueeze(1).to_broadcast([128, 4, 128])),
                             reads=[r_pq[b], r_msk], writes=[r_SC[hd]])
            for hd in range(2):
                for c in range(8):
                    b = (c // 4) % 2
                    S.op("tensor", lambda e, hd=hd, c=c, b=b: e.transpose(pq[b][:, (c % 4) * 128:(c % 4 + 1) * 128],
                                                                          fl(BKp[hd][:, c, :, :]), identf[:, :]),
                         reads=[r_BKp[hd], r_identf], writes=[r_pq[b]], sync_same=False)
                    if c % 4 == 3:
                        h4 = c // 4
                        S.op("scalar", lambda e, hd=hd, h4=h4, b=b: e.copy(
                            out=BKtp[hd][:, h4 * 4:(h4 + 1) * 4, :], in_=pq[b][:, :].rearrange("p (c x) -> p c x", x=128)),
                             reads=[r_pq[b]], writes=[r_BKtp[hd]])
            for hd in range(2 if SUB >= 2 else 0):
                for c in range(8):
                    S.op("tensor", lambda e, hd=hd, c=c: e.matmul(pq[2][0:64, c * 64:(c + 1) * 64], lhsT=BKp[hd][:, c, 1, :], rhs=AR[:, c, 0, :],
                                                                  start=True, stop=True),
                         reads=[r_BKp[hd], r_AR], writes=[r_pq[2]], sync_same=False)
                for c in range(8):
                    S.op("tensor", lambda e, hd=hd, c=c: e.matmul(pq[3][0:64, c * 64:(c + 1) * 64], lhsT=AR[:, c, 0, :], rhs=BKp[hd][:, c, 1, :],
                                                                  start=True, stop=True),
                         reads=[r_BKp[hd], r_AR], writes=[r_pq[3]], sync_same=False)
                p3 = lambda t: t[0:64, :].rearrange("p (c x) -> p c x", x=64)
                S.op("vector", lambda e: e.tensor_mul(out=Pm[:], in0=p3(pq[2]), in1=MU.unsqueeze(1).to_broadcast([64, 8, 64])),
                     reads=[r_pq[2], r_msk], writes=[r_Pm])
                S.op("vector", lambda e: e.tensor_mul(out=Qb[0:64], in0=p3(pq[3]), in1=ML.unsqueeze(1).to_broadcast([64, 8, 64])),
                     reads=[r_pq[3], r_msk], writes=[r_Qb])
                S.op("scalar", lambda e: e.copy(out=PSb[0:64, :, 0, :], in_=Pm[:]), reads=[r_Pm], writes=[r_PSb])
                S.op("vector", lambda e: e.tensor_add(out=SfA[:], in0=Pm[:], in1=IDF64.unsqueeze(1).to_broadcast([64, 8, 64])),
                     reads=[r_Pm, r_identf], writes=[r_SfA])
                S.op("scalar", lambda e: e.copy(out=PSb[0:64, :, 1, :], in_=SfA[:]), reads=[r_SfA], writes=[r_PSb])
                scur = [SfA, r_SfA, SfB, r_SfB]
                for lvl in range(int(os.environ.get('KNLV', '6'))):
                    last = (lvl == 5)
                    for c in range(8):
                        b = c // 4
                        if lvl == 0:
                            S.op("tensor", lambda e, c=c, b=b: e.matmul(pq[b][0:64, (c % 4) * 128:(c % 4) * 128 + 64], lhsT=Qb[:, c, :], rhs=PSb[:, c, 0, :],
                                                                        start=True, stop=True),
                                 reads=[r_Qb, r_PSb], writes=[r_pq[b]], sync_same=False)
                        elif last:
                            S.op("tensor", lambda e, c=c, b=b: e.matmul(pq[b][0:64, (c % 4) * 128 + 64:(c % 4 + 1) * 128], lhsT=Qb[:, c, :], rhs=PSb[:, c, 1, :],
                                                                        start=True, stop=True),
                                 reads=[r_Qb, r_PSb], writes=[r_pq[b]], sync_same=False)
                        else:
                            S.op("tensor", lambda e, c=c, b=b: e.matmul(pq[b][0:64, (c % 4) * 128:(c % 4 + 1) * 128], lhsT=Qb[:, c, :], rhs=fl(PSb[:, c, :, :]),
                                                                        start=True, stop=True),
                                 reads=[r_Qb, r_PSb], writes=[r_pq[b]], sync_same=False)
                    if not last:
                        for c in range(8):
                            S.op("tensor", lambda e, c=c: e.matmul(pq[2][0:64, c * 64:(c + 1) * 64], lhsT=PSb[:, c, 0, :], rhs=Qb[:, c, :],
                                                                   start=True, stop=True),
                                 reads=[r_Qb, r_PSb], writes=[r_pq[2]], sync_same=False)
                    So, r_So, Sn, r_Sn = scur
                    for half in range(2):
                        v = pq[half][0:64, :].rearrange("p (c w x) -> p c w x", w=2, x=64)
                        cs = slice(half * 4, (half + 1) * 4)
                        if lvl > 0 and os.environ.get('KVAR', '0') != '1':
                            if last:
                                S.op("vector", lambda e, v=v, cs=cs, hd=hd, So=So: e.tensor_add(out=Ttp[0:64, cs, hd, 64:128], in0=v[:, :, 1, :], in1=So[:, cs, :]),
                                     reads=[r_So, r_pq[half]], writes=[r_Ttp])
                            else:
                                S.op("vector", lambda e, v=v, cs=cs, So=So, Sn=Sn: e.tensor_add(out=Sn[:, cs, :], in0=v[:, :, 1, :], in1=So[:, cs, :]),
                                     reads=[r_So, r_pq[half]], writes=[r_Sn])
                        if not last:
                            S.op("vector", lambda e, v=v, cs=cs: e.tensor_copy(out=PSb[0:64, cs, 0, :], in_=v[:, :, 0, :]),
                                 reads=[r_pq[half]], writes=[r_PSb])
                    if not last:
                        if lvl > 0:
                            S.op("scalar", lambda e, Sn=Sn: e.copy(out=PSb[0:64, :, 1, :], in_=Sn[:]), reads=[r_Sn], writes=[r_PSb])
                            scur = [Sn, r_Sn, So, r_So]
                        S.op("vector", lambda e: e.tensor_copy(out=Qb[0:64], in_=p3(pq[2])), reads=[r_pq[2]], writes=[r_Qb])

        def seq(g, hcur, want_y, sample_b0=None):
            for c in range(8):
                if sample_b0 is not None:
                    b = sample_b0 + c
                    for hd in range(2):
                        rs = slice(hd * 64, (hd + 1) * 64)
                        S.op("scalar", lambda e, c=c, hd=hd, rs=rs, hcur=hcur: e.copy(out=Hbd[hcur][rs, hd * 64:(hd + 1) * 64], in_=Hin[rs, c, :]),
                             reads=[r_Hin], writes=[r_Hbd[hcur]])
                Hc, rHc = Hbd[hcur], r_Hbd[hcur]
                Hn, rHn = Hbd[1 - hcur], r_Hbd[1 - hcur]
                items = [(lambda e, c=c, Hc=Hc: e.matmul(pq[0][0:64, 0:128], lhsT=AR[:, c, 0, :], rhs=Hc[:, :], start=True, stop=False),
                          [r_AR, rHc], [r_pq[0]])]
                for hd in range(2):
                    items.append((lambda e, c=c, hd=hd: e.matmul(pq[0][0:64, 0:128], lhsT=SC[hd][:, c, 0, :], rhs=UVp[hd][:, c, :],
                                                                 start=False, stop=(hd == 1)),
                                  [r_SC[hd], r_UVp[hd]], [r_pq[0]]))
                S.op_group("tensor", items)
                S.op("scalar", lambda e: e.copy(out=Zs[0:64, :], in_=pq[0][0:64, 0:128]), reads=[r_pq[0]], writes=[r_Zs])
                if int(os.environ.get('KSEQ', '9')) >= 2:
                    if os.environ.get("KNOHG") != "1":
                        S.op("vector", lambda e, c=c, Hc=Hc: e.tensor_scalar(out=HG[:], in0=(identf[:, :] if os.environ.get('KHGI') == '1' else Hc[:, :]), scalar1=(1.0 if os.environ.get("KGC1") == "1" else GC[:, c:c + 1]), scalar2=None, op0=ALU.mult),
                             reads=[rHc, r_GC], writes=[r_HG])
                    for hd in range(2):
                        S.op("tensor", lambda e, c=c, hd=hd: e.matmul(pq[1][:, hd * 64:(hd + 1) * 64], lhsT=Ttp[:, c, hd, :], rhs=Zs[:, hd * 64:(hd + 1) * 64],
                                                                      start=True, stop=True),
                             reads=[r_Ttp, r_Zs], writes=[r_pq[1]], sync_same=False)
                    S.op("scalar", lambda e, c=c: e.copy(out=Up[0][:, c, 0:64], in_=pq[1][:, 0:64]), reads=[r_pq[1]], writes=[r_Up[0]])
                    S.op("scalar", lambda e, c=c: e.copy(out=Up[1][:, c, 64:128], in_=pq[1][:, 64:128]), reads=[r_pq[1]], writes=[r_Up[1]])
                if int(os.environ.get('KSEQ', '9')) >= 3:
                    items = []
                    for hd in range(2):
                        items.append((lambda e, c=c, hd=hd: e.matmul(pq[2][:, 0:128], lhsT=BKtp[hd][:, c, :], rhs=UVp[hd][:, c, :],
                                                                     start=(hd == 0), stop=False),
                                      [r_BKtp[hd], r_UVp[hd]], [r_pq[2]]))
                        items.append((lambda e, c=c, hd=hd: e.matmul(pq[2][:, 0:128], lhsT=BKtp[hd][:, c, :], rhs=Up[hd][:, c, :],
                                                                     start=False, stop=(hd == 1)),
                                      [r_BKtp[hd], r_Up[hd]], [r_pq[2]]))
                    S.op_group("tensor", items)
                if int(os.environ.get('KSEQ', '9')) >= 5:
                    if want_y:
                        items = [(lambda e, c=c, Hc=Hc: e.matmul(pq[3][:, c * 64:(c + 1) * 64], lhsT=Hc[:, :], rhs=AR[:, c, 1, :], start=True, stop=False),
                                  [r_AR, rHc], [r_pq[3]])]
                        for hd in range(2):
                            items.append((lambda e, c=c, hd=hd: e.matmul(pq[3][:, c * 64:(c + 1) * 64], lhsT=UVp[hd][:, c, :], rhs=SC[hd][:, c, 1, :],
                                                                         start=False, stop=False),
                                          [r_SC[hd], r_UVp[hd]], [r_pq[3]]))
                            items.append((lambda e, c=c, hd=hd: e.matmul(pq[3][:, c * 64:(c + 1) * 64], lhsT=Up[hd][:, c, :], rhs=SC[hd][:, c, 1, :],
                                                                         start=False, stop=(hd == 1)),
                                          [r_SC[hd], r_Up[hd]], [r_pq[3]]))
                        S.op_group("tensor", items)
                if int(os.environ.get('KSEQ', '9')) >= 4:
                    S.op("vector", lambda e, c=c, Hn=Hn: e.scalar_tensor_tensor(out=Hn[:, :], in0=pq[2][:, 0:128], scalar=GC[:, c:c + 1], in1=HG[:],
                                                                                op0=ALU.mult, op1=ALU.add),
                         reads=[r_pq[2], r_GC, r_HG], writes=[rHn])
                if sample_b0 is not None:
                    b = sample_b0 + c
                    for hd in range(2):
                        rs = slice(hd * 64, (hd + 1) * 64)
                        S.op("scalar", lambda e, c=c, hd=hd, rs=rs, Hn=Hn: e.copy(out=Hout[rs, c, :], in_=Hn[rs, hd * 64:(hd + 1) * 64]),
                             reads=[rHn], writes=[r_Hout])
                else:
                    hcur = 1 - hcur
            return hcur

        def gn_gate(g, yt, r_yt, bon, r_bon, n, gap, r_gap, dst):
            clg = cvec[:, CV_LG + g:CV_LG + g + 1]
            clb = cvec[:, CV_LB + g:CV_LB + g + 1]
            S.op("tensor", lambda e: e.matmul(pq[0][:, 0:n], lhsT=BONES, rhs=yt, start=True, stop=True),
                 reads=[r_msk, r_yt], writes=[r_pq[0]], sync_same=False)
            S.op("scalar", lambda e: e.activation(out=G1[:, 0:n], in_=yt, func=AF.Square), reads=[r_yt], writes=[r_G1])
            S.op("tensor", lambda e: e.matmul(pq[1][:, 0:n], lhsT=BONES, rhs=G1[:, 0:n], start=True, stop=True),
                 reads=[r_msk, r_G1], writes=[r_pq[1]], sync_same=False)
            S.op("vector", lambda e: e.tensor_scalar(out=G2[:, 0:n], in0=pq[0][:, 0:n], scalar1=1.0 / 64, scalar2=None, op0=ALU.mult),
                 reads=[r_pq[0]], writes=[r_G2])
            S.op("vector", lambda e: e.tensor_mul(out=G3[:, 0:n], in0=G2[:, 0:n], in1=G2[:, 0:n]), reads=[r_G2], writes=[r_G3])
            S.op("vector", lambda e: e.scalar_tensor_tensor(out=KF[:, 0:n], in0=pq[1][:, 0:n], scalar=1.0 / 64, in1=G3[:, 0:n],
                                                            op0=ALU.mult, op1=ALU.subtract),
                 reads=[r_pq[1], r_G3], writes=[r_KF])
            S.op("scalar", lambda e: e.activation(out=G3[:, 0:n], in_=KF[:, 0:n], func=AF.Sqrt, bias=64e-5), reads=[r_KF], writes=[r_G3])
            S.op("vector", lambda e: e.reciprocal(out=G3[:, 0:n], in_=G3[:, 0:n]), reads=[r_G3], writes=[r_G3])
            S.op("vector", lambda e: e.tensor_sub(out=G1[:, 0:n], in0=yt, in1=G2[:, 0:n]), reads=[r_yt, r_G2], writes=[r_G1])
            S.op("vector", lambda e: e.tensor_mul(out=G1[:, 0:n], in0=G1[:, 0:n], in1=G3[:, 0:n]), reads=[r_G1, r_G3], writes=[r_G1])
            S.op("vector", lambda e: e.tensor_scalar(out=G1[:, 0:n], in0=G1[:, 0:n], scalar1=clg, scalar2=clb, op0=ALU.mult, op1=ALU.add),
                 reads=[r_G1, r_cvec], writes=[r_G1])
            S.op("vector", lambda e: e.tensor_add(out=G1[:, 0:n], in0=G1[:, 0:n], in1=bon), reads=[r_G1, r_bon], writes=[r_G1])
            S.op("scalar", lambda e: e.activation(out=G2[:, 0:n], in_=gap, func=AF.Silu), reads=[r_gap], writes=[r_G2])
            S.op("vector", lambda e: e.tensor_mul(out=dst, in0=G1[:, 0:n], in1=G2[:, 0:n]), reads=[r_G1, r_G2], writes=[r_og[g]])

        ci = plan[(ps, "xwxa")]
        toks = [(0, 512, pg[0][:, :], r_pg[0]), (512, 512, pg[1][:, :], r_pg[1])]
        if has_s:
            toks.append((1024, NS, pgs[:, 0:NS], r_pgs))
        gemmF(ci, xnT, r_xnT, toks)
        wdone(ci)
        for tt in range(2):
            shift_evac("x", 24, pg[tt][:, :], r_pg[tt], 512, LWIN[:, tt * 512:(tt + 1) * 512], r_LWIN)
        if has_s:
            shift_evac_sample(24, pgs[:, 0:NS], r_pgs, LWIN[:, 1024:1024 + NS], r_LWIN)
        nlw = 1024 + NS if has_s else 1024
        S.op("scalar", lambda e: e.activation(out=LWIN[0:64, 0:nlw], in_=LWIN[0:64, 0:nlw], func=AF.Tanh),
             reads=[r_LWIN], writes=[r_LWIN])
        for g in range(8):
            hcur = 0
            S.op("vector", lambda e, g=g: e.tensor_copy(out=Hbd[0][:, :], in_=Hsave[:, g, :]), reads=[r_Hsave[g]], writes=[r_Hbd[0]])
            for tt in range(2):
                for j, (nm, dst, rd) in enumerate((("r", XR, r_XR), ("k", XK, r_XK), ("v", XV, r_XV))):
                    ci = plan[(ps, nm, g)]
                    bi = (tt * 3 + j) % 2
                    gemmF(ci, xnT, r_xnT, [(tt * 512, 512, pg[bi][:, :], r_pg[bi])])
                    ci25 = {"r": 0, "k": 8, "v": 16}[nm] + g
                    shift_evac(nm, ci25, pg[bi][:, :], r_pg[bi], 512, dst[:, :], rd)
                if stage >= 2:
                    prep(g, tt * 512, False, has_s)
                    if SUB >= 3:
                        hcur = seq(g, hcur, has_s)
                if has_s and stage >= 3:
                    S.op("scalar", lambda e: e.copy(out=YT[:], in_=pq[3][:, :]), reads=[r_pq[3]], writes=[r_YT])
                    gemmF(plan[("ga", g)], xnT, r_xnT, [(tt * 512, 512, pg[0][:, :], r_pg[0])])
                    gn_gate(g, YT[:], r_YT, BON[:], r_BON, 512, pg[0][:, :], r_pg[0], ogT[:, g, tt * 512:(tt + 1) * 512])
            if stage >= 2:
                if ps == "P":
                    S.op("vector", lambda e, g=g, hcur=hcur: e.tensor_copy(out=Hsave[:, g, :], in_=Hbd[hcur][:, :]),
                         reads=[r_Hbd[hcur]], writes=[r_Hsave[g]])
                else:
                    for hd in range(2):
                        rs = slice(hd * 64, (hd + 1) * 64)
                        S.op("vector", lambda e, g=g, hcur=hcur, hd=hd, rs=rs: e.tensor_copy(out=nwp_sb[rs, g, :], in_=Hbd[hcur][rs, hd * 64:(hd + 1) * 64]),
                             reads=[r_Hbd[hcur]], writes=[r_nwp])
            if has_s:
                for j, (nm, dst, rd) in enumerate((("r", XRs, r_XRs), ("k", XKs, r_XKs), ("v", XVs, r_XVs))):
                    ci = plan[(ps, nm, g)]
                    gemmF(ci, xnT, r_xnT, [(1024, NS, pgs[:, j * NS:(j + 1) * NS], r_pgs)])
                    ci25 = {"r": 0, "k": 8, "v": 16}[nm] + g
                    shift_evac_sample(ci25, pgs[:, j * NS:(j + 1) * NS], r_pgs, dst[:, :], rd)
                if stage >= 2 and SUB >= 4:
                    S.op("tensor", lambda e, g=g: e.matmul(pq[0][:, 0:NS], lhsT=lwp[:, 0, g * 128:(g + 1) * 128], rhs=LWIN[:, 1024:1024 + NS],
                                                           start=True, stop=True), reads=[r_lwp, r_LWIN], writes=[r_pq[0]], sync_same=False)
                    S.op("tensor", lambda e, g=g: e.matmul(pq[1][:, 0:NS], lhsT=lwp[:, 1, g * 128:(g + 1) * 128], rhs=LWIN[:, 1024:1024 + NS],
                                                           start=True, stop=True), reads=[r_lwp, r_LWIN], writes=[r_pq[1]], sync_same=False)
                    S.op("scalar", lambda e, g=g: e.activation(out=LDs[:], in_=pq[0][:, 0:NS], func=AF.Sigmoid, bias=cvec[:, CV_W0 + g:CV_W0 + g + 1]),
                         reads=[r_pq[0], r_cvec], writes=[r_LDs])
                    S.op("scalar", lambda e, g=g: e.activation(out=AAs[:], in_=pq[1][:, 0:NS], func=AF.Sigmoid, bias=cvec[:, CV_A0 + g:CV_A0 + g + 1]),
                         reads=[r_pq[1], r_cvec], writes=[r_AAs])
                    for st in range(2):
                        S.dma("sync", Hin[:], wkvT_d[st * 8:(st + 1) * 8, g, :, :].rearrange("b p i -> p b i"), writes=[r_Hin])
                        for (tile, rt, src, rsrc) in ((XR, r_XR, XRs, r_XRs), (XK, r_XK, XKs, r_XKs), (XV, r_XV, XVs, r_XVs),
                                                      (LD, r_LD, LDs, r_LDs), (AA, r_AA, AAs, r_AAs)):
                            S.op(POOL, lambda e, tile=tile: e.memset(tile[:], 0.0), writes=[rt])
                            S.op(POOL, lambda e, tile=tile, src=src, st=st: e.tensor_copy(out=c3(tile)[:, :, 0], in_=src[:, st * 8:(st + 1) * 8]),
                                 reads=[rsrc], writes=[rt])
                        prep(g, 0, True, True)
                        if os.environ.get('KSAMP') != '1':
                            seq(g, 0, True, sample_b0=st * 8)
                            S.dma("sync", nws_d[st * 8:(st + 1) * 8, g, :, :].rearrange("b p i -> p b i"), Hout[:], reads=[r_Hout], writes=[out_regs["nws"]])
                        if stage >= 3:
                            S.op("scalar", lambda e, st=st: e.copy(out=YTs[:, st * 8:(st + 1) * 8], in_=c3(pq[3])[:, :, 0]), reads=[r_pq[3]], writes=[r_YTs])
                            S.op("vector", lambda e, st=st: e.tensor_copy(out=BONs[:, st * 8:(st + 1) * 8], in_=c3(BON)[:, :, 0]), reads=[r_BON], writes=[r_BONs])
                    if stage >= 3:
                        gemmF(plan[("ga", g)], xnT, r_xnT, [(1024, NS, pgs[:, 64:64 + NS], r_pgs)])
                        gn_gate(g, YTs[:], r_YTs, BONs[:], r_BONs, NS, pgs[:, 64:64 + NS], r_pgs, ogT[:, g, 1024:1024 + NS])
            for nm in ("r", "k", "v"):
                wdone(plan[(ps, nm, g)])
            if has_s:
                wdone(plan[("ga", g)])

    run_pass("P")
    run_pass("O")
    S.dma("sync", nsp_d, carry[:], reads=[r_carry], writes=[out_regs["nsp"]])
    S.dma("sync", nss_d, nss_sb[:], reads=[r_nss], writes=[out_regs["nss"]])
    S.dma("sync", nwp_d, nwp_sb[:], reads=[r_nwp], writes=[out_regs["nwp"]])

    reset_union()

    def ub(name, shape, dt=F32):
        return sb(name, shape, dt, reg="U"), Reg(name)

    U, _ = ub("U", [128, 8, 30 + TOWN]); r_U = [Reg("U%d" % c) for c in range(8)]
    ucs, r_ucs = ub("ucs", [128, 8, NS, 31])
    S.dma("sync", ucs[:, :, :, 0:30], scT_d, writes=[r_ucs])
    GA, r_GA = ub("GA", [128, 32 + TOWN + NS])
    SG, r_SG = ub("SG", [128, 32 + TOWN + NS])
    for c in range(8):
        for part in ("glua", "glub"):
            ci = plan[(part, c)]
            gemmF(ci, xnT, r_xnT, [(0, 512, pg[0][:, :], r_pg[0]), (512, 512, pg[1][:, :], r_pg[1]),
                                   (1024, NS, pgs[:, 0:NS], r_pgs)])
            gemmF(ci, xhist, r_xhist, [(0, 32, pgs[:, 32:64], r_pgs)])
            wdone(ci)
            dstb, rdst = (GA, r_GA) if part == "glua" else (SG, r_SG)
            fn = AF.Copy if part == "glua" else AF.Sigmoid
            for (src, preg, c0, n) in ((pgs[:, 32:64], r_pgs, 0, 32), (pg[0][:, :], r_pg[0], 32, 512),
                                       (pg[1][:, :], r_pg[1], 544, 512), (pgs[:, 0:NS], r_pgs, 1056, NS)):
                S.op("scalar", lambda e, src=src, c0=c0, n=n, dstb=dstb, fn=fn: e.activation(out=dstb[:, c0:c0 + n], in_=src, func=fn),
                     reads=[preg], writes=[rdst])
        S.op("vector", lambda e, c=c: e.tensor_mul(out=U[:, c, :], in0=GA[:, 2:32 + TOWN], in1=SG[:, 2:32 + TOWN]),
             reads=[r_GA, r_SG], writes=[r_U[c]])
        S.op("vector", lambda e, c=c: e.tensor_mul(out=ucs[:, c, :, 30], in0=GA[:, 1056:1056 + NS], in1=SG[:, 1056:1056 + NS]),
             reads=[r_GA, r_SG], writes=[r_ucs])
    S.dma("sync", ncp_d, U[:, :, TOWN:TOWN + 30], reads=r_U, writes=[out_regs["ncp"]])
    S.dma("sync", ncs_d, ucs[:, :, :, 1:31], reads=[r_ucs], writes=[out_regs["ncs"]])


    if stage >= 3:
        u_addr = base + PSZ
        Cb, _ = ub("C", [128, 8, XW]); r_C = [Reg("C%d" % c) for c in range(8)]
        MEAN, r_MEAN = ub("MEAN", [128, 512]); RSTD, r_RSTD = ub("RSTD", [128, 512]); SQ, r_SQ = ub("SQ", [128, 512])
        stm, r_stm = ub("stm", [128, NS, 31])
        for c in range(8):
            cw = lambda k, c=c: cvec[:, CV_CW + c * 31 + k:CV_CW + c * 31 + k + 1]
            S.op("vector", lambda e, c=c, cw=cw: e.tensor_scalar(out=Cb[:, c, 0:TOWN], in0=U[:, c, 0:TOWN], scalar1=cw(0),
                                                                scalar2=cvec[:, CV_CB + c:CV_CB + c + 1], op0=ALU.mult, op1=ALU.add),
                 reads=[r_U[c], r_cvec], writes=[r_C[c]])
            for k in range(1, 31):
                S.op("vector", lambda e, c=c, k=k, cw=cw: e.scalar_tensor_tensor(out=Cb[:, c, 0:TOWN], in0=U[:, c, k:k + TOWN], scalar=cw(k),
                                                                              in1=Cb[:, c, 0:TOWN], op0=ALU.mult, op1=ALU.add),
                     reads=[r_U[c], r_cvec, r_C[c]], writes=[r_C[c]])
            S.op("vector", lambda e, c=c: e.tensor_mul(out=stm[:], in0=ucs[:, c, :, :],
                                                       in1=cvec[:, CV_CW + c * 31:CV_CW + (c + 1) * 31].unsqueeze(1).to_broadcast([128, NS, 31])),
                 reads=[r_ucs, r_cvec], writes=[r_stm])
            S.op("vector", lambda e, c=c: e.tensor_reduce(out=Cb[:, c, TOWN:XW], in_=stm[:], axis=AX.X, op=ALU.add),
                 reads=[r_stm], writes=[r_C[c]])
            S.op("vector", lambda e, c=c: e.tensor_scalar(out=Cb[:, c, TOWN:XW], in0=Cb[:, c, TOWN:XW], scalar1=cvec[:, CV_CB + c:CV_CB + c + 1],
                                                          scalar2=None, op0=ALU.add), reads=[r_C[c], r_cvec], writes=[r_C[c]])
        cbT = nc.alloc_sbuf_tensor_at("cbT_alias", [128, 8, XW], BF16, offset=u_addr)
        r_cb = [Reg("cb%d" % c) for c in range(8)]
        tiles3 = [(0, 512), (512, 512), (1024, NS)]
        for (c0, n) in tiles3:
            for c in range(8):
                S.op("tensor", lambda e, c=c, c0=c0, n=n: e.matmul(pq[0][:, 0:n], lhsT=ONES, rhs=Cb[:, c, c0:c0 + n], start=(c == 0), stop=(c == 7)),
                     reads=[r_msk, r_C[c]], writes=[r_pq[0]], sync_same=False)
            for c in range(8):
                S.op("scalar", lambda e, c=c, c0=c0, n=n: e.activation(out=SQ[:, 0:n], in_=Cb[:, c, c0:c0 + n], func=AF.Square),
                     reads=[r_C[c]], writes=[r_SQ])
                S.op("tensor", lambda e, c=c, n=n: e.matmul(pq[1][:, 0:n], lhsT=ONES, rhs=SQ[:, 0:n], start=(c == 0), stop=(c == 7)),
                     reads=[r_msk, r_SQ], writes=[r_pq[1]], sync_same=False)
            S.op("vector", lambda e, n=n: e.tensor_scalar(out=MEAN[:, 0:n], in0=pq[0][:, 0:n], scalar1=1.0 / DA, scalar2=None, op0=ALU.mult),
                 reads=[r_pq[0]], writes=[r_MEAN])
            S.op("vector", lambda e, n=n: e.tensor_mul(out=SQ[:, 0:n], in0=MEAN[:, 0:n], in1=MEAN[:, 0:n]), reads=[r_MEAN], writes=[r_SQ])
            S.op("vector", lambda e, n=n: e.scalar_tensor_tensor(out=RSTD[:, 0:n], in0=pq[1][:, 0:n], scalar=1.0 / DA, in1=SQ[:, 0:n],
                                                                 op0=ALU.mult, op1=ALU.subtract), reads=[r_pq[1], r_SQ], writes=[r_RSTD])
            S.op("scalar", lambda e, n=n: e.activation(out=RSTD[:, 0:n], in_=RSTD[:, 0:n], func=AF.Sqrt, bias=1e-5), reads=[r_RSTD], writes=[r_RSTD])
            S.op("vector", lambda e, n=n: e.reciprocal(out=RSTD[:, 0:n], in_=RSTD[:, 0:n]), reads=[r_RSTD], writes=[r_RSTD])
            for c in range(8):
                S.op("vector", lambda e, c=c, c0=c0, n=n: e.tensor_sub(out=Cb[:, c, c0:c0 + n], in0=Cb[:, c, c0:c0 + n], in1=MEAN[:, 0:n]),
                     reads=[r_C[c], r_MEAN], writes=[r_C[c]])
                S.op("vector", lambda e, c=c, c0=c0, n=n: e.tensor_mul(out=Cb[:, c, c0:c0 + n], in0=Cb[:, c, c0:c0 + n], in1=RSTD[:, 0:n]),
                     reads=[r_C[c], r_RSTD], writes=[r_C[c]])
                S.op("scalar", lambda e, c=c, c0=c0, n=n: e.activation(out=Cb[:, c, c0:c0 + n], in_=Cb[:, c, c0:c0 + n], func=AF.Silu,
                                                                       scale=cvec[:, CV_CG + c:CV_CG + c + 1], bias=cvec[:, CV_CLB + c:CV_CLB + c + 1]),
                     reads=[r_C[c], r_cvec], writes=[r_C[c]])
        for c in range(8):
            ci = plan[("gb", c)]
            gemmF(ci, xnT, r_xnT, [(0, 512, pg[0][:, :], r_pg[0]), (512, 512, pg[1][:, :], r_pg[1]), (1024, NS, pgs[:, 0:NS], r_pgs)])
            wdone(ci)
            for (c0, n, src, preg) in ((0, 512, pg[0][:, :], r_pg[0]), (512, 512, pg[1][:, :], r_pg[1]), (1024, NS, pgs[:, 0:NS], r_pgs)):
                S.op("scalar", lambda e, n=n, src=src: e.activation(out=SQ[:, 0:n], in_=src, func=AF.Silu), reads=[preg], writes=[r_SQ])
                S.op("vector", lambda e, c=c, c0=c0, n=n: e.tensor_mul(out=cbT[:, c, c0:c0 + n], in0=Cb[:, c, c0:c0 + n], in1=SQ[:, 0:n]),
                     reads=[r_C[c], r_SQ], writes=[r_cb[c]] + r_U + [out_regs["ncp"]])

        reset_union()
        _cb2, _ = ub("cbT2", [128, 8, XW + 7], BF16)
        mT, _ = ub("mT", [128, 16, XW], BF16); r_mT = [Reg("mT%d" % d) for d in range(16)]
        YA, r_YA = ub("YA", [128, XW]); YB, r_YB = ub("YB", [128, XW]); M1, r_M1 = ub("M1", [128, XW]); M2, r_M2 = ub("M2", [128, XW])
        tg = [(0, 512, pg[0][:, :], r_pg[0]), (512, 512, pg[1][:, :], r_pg[1]), (1024, NS, pgs[:, 0:NS], r_pgs)]
        r_ogall = Reg("ogall"); r_cball = Reg("cball")
        for dc in range(16):
            for (key, rhs, rr, dstb, rd, fn) in ((("pa", dc), ogT, r_ogall, YA, r_YA, AF.Copy), (("mga", dc), xnT, r_xnT, M1, r_M1, AF.Sigmoid),
                                                 (("pb", dc), cbT, r_cball, YB, r_YB, AF.Copy), (("mgb", dc), xnT, r_xnT, M2, r_M2, AF.Sigmoid)):
                ci = plan[key]
                gemmF(ci, rhs, rr, tg)
                wdone(ci)
                for (c0, n, src, preg) in tg:
                    S.op("scalar", lambda e, c0=c0, n=n, src=src, dstb=dstb, fn=fn: e.activation(out=dstb[:, c0:c0 + n], in_=src, func=fn),
                         reads=[preg], writes=[rd])
            S.op("vector", lambda e: e.tensor_mul(out=M1[:], in0=M1[:], in1=YA[:]), reads=[r_M1, r_YA], writes=[r_M1])
            S.op("vector", lambda e: e.tensor_mul(out=M2[:], in0=M2[:], in1=YB[:]), reads=[r_M2, r_YB], writes=[r_M2])
            S.op("vector", lambda e, dc=dc: e.tensor_add(out=mT[:, dc, :], in0=M1[:], in1=M2[:]), reads=[r_M1, r_M2], writes=[r_mT[dc]])

        reset_union()
        _keep, _ = ub("keep", [128, (8 * (XW + 7) * 2 + 16 * XW * 2 + 255) // 256 * 64])
        r_mTall = Reg("mTall")
        fgT = ub("fgT", [128, 16])[0]; r_fgT = Reg("fgT")
        S.dma("sync", fgT[:], fgT_d, writes=[r_fgT])
        hT, r_hT = ub("hT", [128, 16, 528])
        xtl, r_xtl = ub("xtl", [128, D])
        TG, r_TG = ub("TG", [128, 528]); RS, r_RS = ub("RS", [128, 528])
        ptl, r_ptl = ub("ptl", [128, 256])
        hTb, r_hTb = xnT, Reg("hTb")
        pT, r_pT = ogT, Reg("pT")
        for hh in range(2):
            t0h = 0 if hh == 0 else 512
            nh = 512 if hh == 0 else 512 + NS
            ntile = (nh + 127) // 128
            for ti in range(ntile):
                n = min(128, nh - ti * 128)
                row = 1024 + t0h + ti * 128
                S.dma("sync", xtl[0:n, :], xs_d[row:row + n, :], writes=[r_xtl])
                S.dma("sync", ptl[0:n, :], pp_d[t0h + ti * 128:t0h + ti * 128 + n, :], writes=[r_ptl])
                for q4 in range(4):
                    for j in range(4):
                        ec = q4 * 4 + j
                        S.op("tensor", lambda e, n=n, ec=ec, j=j: e.transpose(pq[0][:, j * 128:j * 128 + n], xtl[0:n, ec * 128:(ec + 1) * 128], identf[0:n, 0:n]),
                             reads=[r_xtl, r_identf], writes=[r_pq[0]], sync_same=False)
                    S.op("scalar", lambda e, n=n, q4=q4, ti=ti: e.copy(out=hT[:, q4 * 4:(q4 + 1) * 4, ti * 128:ti * 128 + n],
                                                                     in_=pq[0][:, :].rearrange("p (c t) -> p c t", t=128)[:, :, 0:n]),
                         reads=[r_pq[0]], writes=[r_hT])
                for j in range(2):
                    S.op("tensor", lambda e, n=n, j=j: e.transpose(pq[1][:, j * 128:j * 128 + n], ptl[0:n, j * 128:(j + 1) * 128], identf[0:n, 0:n]),
                         reads=[r_ptl, r_identf], writes=[r_pq[1]], sync_same=False)
                S.op("scalar", lambda e, n=n, ti=ti: e.copy(out=pT[:, 0:2, ti * 128:ti * 128 + n],
                                                          in_=pq[1][:, 0:256].rearrange("p (c t) -> p c t", t=128)[:, :, 0:n]),
                     reads=[r_pq[1]], writes=[r_pT])
            tgh = [(t0h, 512, pg[0][:, :], r_pg[0])] + ([(1024, NS, pgs[:, 0:NS], r_pgs)] if hh == 1 else [])
            loc = [(0, 512)] + ([(512, NS)] if hh == 1 else [])
            for ec in range(16):
                ci = plan[("wo", ec, hh)]
                gemmF(ci, mT, r_mTall, tgh)
                wdone(ci)
                for (c0, n, src, preg), (l0, ln) in zip(tgh, loc):
                    S.op("vector", lambda e, ec=ec, src=src, l0=l0, ln=ln: e.tensor_add(out=hT[:, ec, l0:l0 + ln], in0=src, in1=hT[:, ec, l0:l0 + ln]),
                         reads=[preg, r_hT], writes=[r_hT])
            S.op("scalar", lambda e, nh=nh: e.copy(out=hTb[:, :, 0:nh], in_=hT[:, :, 0:nh]), reads=[r_hT], writes=[r_hTb])
            tgl = [(0, 512, pg[0][:, :], r_pg[0])] + ([(512, NS, pgs[:, 0:NS], r_pgs)] if hh == 1 else [])
            tgp = [(0, 512, pg[1][:, :], r_pg[1])] + ([(512, NS, pgs[:, 64:64 + NS], r_pgs)] if hh == 1 else [])
            for fc in range(16):
                ci = plan[("pgt", fc, hh)]
                gemmF(ci, hTb, r_hTb, tgl)
                wdone(ci)
                ci = plan[("ple", fc, hh)]
                gemmF(ci, pT, r_pT, tgp)
                wdone(ci)
                for (c0, n, src, preg), (_, _, srcp, pregp) in zip(tgl, tgp):
                    S.op("scalar", lambda e, c0=c0, n=n, src=src: e.activation(out=TG[:, c0:c0 + n], in_=src, func=AF.Sigmoid), reads=[preg], writes=[r_TG])
                    S.op("scalar", lambda e, c0=c0, n=n, srcp=srcp: e.copy(out=RS[:, c0:c0 + n], in_=srcp), reads=[pregp], writes=[r_RS])
                S.op("vector", lambda e, nh=nh: e.tensor_mul(out=TG[:, 0:nh], in0=TG[:, 0:nh], in1=RS[:, 0:nh]), reads=[r_TG, r_RS], writes=[r_TG])
                S.op("vector", lambda e, fc=fc, nh=nh: e.tensor_add(out=hT[:, fc, 0:nh], in0=hT[:, fc, 0:nh], in1=TG[:, 0:nh]), reads=[r_hT, r_TG], writes=[r_hT])
            for (l0, ln) in loc:
                for fc in range(16):
                    S.op("scalar", lambda e, fc=fc, l0=l0, ln=ln: e.activation(out=TG[:, 0:ln], in_=hT[:, fc, l0:l0 + ln], func=AF.Square),
                         reads=[r_hT], writes=[r_TG])
                    S.op("tensor", lambda e, fc=fc, ln=ln: e.matmul(pq[2][:, 0:ln], lhsT=ONES, rhs=TG[:, 0:ln], start=(fc == 0), stop=(fc == 15)),
                         reads=[r_msk, r_TG], writes=[r_pq[2]], sync_same=False)
                S.op("scalar", lambda e, l0=l0, ln=ln: e.activation(out=RS[:, l0:l0 + ln], in_=pq[2][:, 0:ln], func=AF.Sqrt, scale=1.0 / D, bias=1e-6),
                     reads=[r_pq[2]], writes=[r_RS])
            S.op("vector", lambda e, nh=nh: e.reciprocal(out=RS[:, 0:nh], in_=RS[:, 0:nh]), reads=[r_RS], writes=[r_RS])
            for fc in range(16):
                S.op("vector", lambda e, fc=fc, nh=nh: e.scalar_tensor_tensor(out=hT[:, fc, 0:nh], in0=hT[:, fc, 0:nh], scalar=fgT[:, fc:fc + 1],
                                                                            in1=RS[:, 0:nh], op0=ALU.mult, op1=ALU.mult),
                     reads=[r_hT, r_RS, r_fgT], writes=[r_hT])
            for ti in range(ntile):
                n = min(128, nh - ti * 128)
                for q4 in range(4):
                    for j in range(4):
                        fc = q4 * 4 + j
                        S.op("tensor", lambda e, n=n, fc=fc, j=j, ti=ti: e.transpose(pq[3][0:n, j * 128:(j + 1) * 128], hT[:, fc, ti * 128:ti * 128 + n], identf[:, :]),
                             reads=[r_hT, r_identf], writes=[r_pq[3]], sync_same=False)
                    S.op("scalar", lambda e, n=n, q4=q4: e.copy(out=xtl[0:n, q4 * 512:(q4 + 1) * 512], in_=pq[3][0:n, :]), reads=[r_pq[3]], writes=[r_xtl])
                orow = t0h + ti * 128
                S.dma("sync", y_d[orow:orow + n, :], xtl[0:n, :], reads=[r_xtl], writes=[out_regs["y"]])

    for _i in range(int(os.environ.get("KDUMMY", "0"))):
        S.op("tensor", lambda e: e.matmul(pq[3][:, 0:128], lhsT=identf[:, :], rhs=identf[:, :], start=True, stop=True),
             reads=[r_identf], writes=[r_pq[3]], sync_same=False)
    deps = {}
    for r in out_regs.values():
        if r.w is not None:
            deps[r.w[0]] = max(deps.get(r.w[0], 0), r.w[1])
    S._emit_waits("sync", deps)
    S.emit()
    return nc, S


_CACHE = {}


def _host_inputs(inp):
    f32 = np.float32
    x_prompt = np.asarray(inp["x_prompt"], f32)
    x_sample = np.asarray(inp["x_sample"], f32)[:, 0, :]
    p_prompt = np.asarray(inp["p_prompt"], f32)[0]
    p_sample = np.asarray(inp["p_sample"], f32)[0][:, 0, :]
    state_shift = np.asarray(inp["state_shift"], f32)[0]
    state_wkv = np.asarray(inp["state_wkv"], f32)[0]
    state_conv = np.asarray(inp["state_conv"], f32)[0]

    def fm(v, n):
        return np.ascontiguousarray(np.asarray(v, f32).reshape(n, 128).T)

    cvec = np.zeros((128, CV_N), f32)
    cvec[:, CV_MU:CV_MU + 25] = fm(inp["shift_mu"][0], 25)
    for off, key in ((CV_W0, "w0"), (CV_A0, "a0"), (CV_KK, "k_k"), (CV_KA, "k_a"), (CV_LG, "lnx_g"), (CV_LB, "lnx_b"),
                     (CV_CB, "conv_b"), (CV_CG, "cln_g"), (CV_CLB, "cln_b")):
        cvec[:, off:off + 8] = fm(inp[key][0], 8)
    cvec[:, CV_RK:CV_RK + 8] = fm(np.asarray(inp["r_k"], f32)[0].reshape(-1), 8)
    cw = np.asarray(inp["conv_w"], f32)[0]
    cvec[:, CV_CW:CV_CW + 248] = cw.reshape(31, 8, 128).transpose(2, 1, 0).reshape(128, 248)
    lw = np.ascontiguousarray(np.concatenate([np.asarray(inp["w_lora_b"], f32)[0], np.asarray(inp["a_lora_b"], f32)[0]], 0))
    identb = np.eye(128, dtype=f32).astype(ml_dtypes.bfloat16)
    identf = np.eye(128, dtype=f32)
    msk = np.zeros((128, MK_N), f32)
    sidx = np.arange(128) % 64
    tidx = np.arange(64)
    msk[:, MK_SC:MK_SC + 64] = (tidx[None, :] > sidx[:, None])
    msk[:, MK_SC + 64:MK_SC + 128] = (tidx[None, :] >= sidx[:, None])
    msk[0:64, MK_MU:MK_MU + 64] = (tidx[None, :] > tidx[:, None])
    msk[0:64, MK_ML:MK_ML + 64] = (tidx[None, :] < tidx[:, None])
    msk[:, MK_RM:MK_RM + 512] = (np.arange(512) % 64 != 0)[None, :]
    msk[:, MK_BO:MK_BO + 128] = ((np.arange(128) // 64)[:, None] == (np.arange(128) // 64)[None, :])
    msk[:, MK_ON:MK_ON + 128] = 1.0
    lwp = np.zeros((128, 2, DA), f32)
    lwp[0:64, 0] = np.asarray(inp["w_lora_b"], f32)[0]
    lwp[64:128, 1] = np.asarray(inp["a_lora_b"], f32)[0]
    shared = dict(
        w_in=np.ascontiguousarray(np.asarray(inp["w_in"], f32)[0]),
        w_pa=np.ascontiguousarray(np.asarray(inp["w_proj_a"], f32)[0]),
        w_pb=np.ascontiguousarray(np.asarray(inp["w_proj_b"], f32)[0]),
        w_out=np.ascontiguousarray(np.asarray(inp["w_out"], f32)[0]),
        w_pg=np.ascontiguousarray(np.asarray(inp["w_ple_gate"], f32)[0]),
        w_ple=np.ascontiguousarray(np.asarray(inp["w_ple"], f32)[0]),
        ng=np.ascontiguousarray(np.asarray(inp["norm_g"], f32).reshape(1, D)),
        fg=np.ascontiguousarray(np.asarray(inp["final_g"], f32).reshape(1, D)),
        fgT=np.ascontiguousarray(np.asarray(inp["final_g"], f32).reshape(16, 128).T),
        cvec=cvec, lwp=lwp, identb=identb, identf=identf, msk=msk,
    )
    maps = []
    for c in range(NCORE):
        b, hf = c // 2, c % 2
        own = x_prompt[b, hf * 1024:(hf + 1) * 1024]
        pre = x_prompt[b, 0:1024] if hf == 1 else np.zeros((1024, D), f32)
        sm = slice(c * NS, (c + 1) * NS)
        m = dict(shared)
        m["xs"] = np.ascontiguousarray(np.concatenate([pre, own, x_sample[sm]], 0))
        m["ssT"] = np.ascontiguousarray(state_shift[sm].T.reshape(25, 128, NS).transpose(1, 0, 2))
        m["wkvT"] = np.ascontiguousarray(state_wkv[sm].reshape(NS, 8, 2, 64, 64).transpose(0, 1, 2, 4, 3).reshape(NS, 8, 128, 64))
        m["scT"] = np.ascontiguousarray(state_conv[sm].transpose(2, 0, 1).reshape(8, 128, NS, 30).transpose(1, 0, 2, 3))
        m["pp"] = np.ascontiguousarray(np.concatenate([p_prompt[b, hf * 1024:(hf + 1) * 1024], p_sample[sm]], 0))
        maps.append(m)
    return maps


def _assemble(res):
    f32 = np.float32
    y_prompt = np.zeros((4, 2048, D), f32)
    y_sample = np.zeros((128, 1, D), f32)
    nsp = np.zeros((1, 4, SHIFT_W), f32)
    nwp = np.zeros((1, 4, 16, 64, 64), f32)
    ncp = np.zeros((1, 4, 30, DA), f32)
    nss = np.zeros((1, 128, SHIFT_W), f32)
    nws = np.zeros((1, 128, 16, 64, 64), f32)
    ncs = np.zeros((1, 128, 30, DA), f32)
    for c in range(NCORE):
        r = res[c]
        b, hf = c // 2, c % 2
        sm = slice(c * NS, (c + 1) * NS)
        y_prompt[b, hf * 1024:(hf + 1) * 1024] = r["y"][0:1024]
        y_sample[sm, 0] = r["y"][1024:1024 + NS]
        nss[0, sm] = r["nss"].transpose(1, 0, 2).reshape(SHIFT_W, NS).T
        nws[0, sm] = r["nws"].reshape(NS, 8, 2, 64, 64).transpose(0, 1, 2, 4, 3).reshape(NS, 16, 64, 64)
        ncs[0, sm] = r["ncs"].transpose(1, 0, 2, 3).reshape(DA, NS, 30).transpose(1, 2, 0)
        if hf == 1:
            nsp[0, b] = r["nsp"].T.reshape(SHIFT_W)
            nwp[0, b] = r["nwp"].reshape(2, 64, 8, 64).transpose(2, 0, 3, 1).reshape(16, 64, 64)
            ncp[0, b] = r["ncp"].transpose(1, 0, 2).reshape(DA, 30).T
    return (y_prompt, y_sample, nsp, nwp, ncp, nss, nws, ncs)


def kernel(**inputs):
    maps = _host_inputs(inputs)
    if "nc" not in _CACHE:
        _CACHE["nc"] = build(int(os.environ.get("KSTAGE", "99")))[0]
    nc = _CACHE["nc"]
    res = run_bass_kernel_spmd(nc, maps, core_ids=list(range(NCORE)))
    return _assemble(res.results)
```

```python
import os
import numpy as np
import ml_dtypes
import concourse.bass as bass
import concourse.mybir as mybir
from concourse.bass_utils import run_bass_kernel_spmd

F32 = mybir.dt.float32
BF16 = mybir.dt.bfloat16
AF = mybir.ActivationFunctionType
ALU = mybir.AluOpType
AX = mybir.AxisListType

D = 2048
DA = 1024
SHIFT_W = 3200
O1 = 3200
O2 = O1 + 1024
O3 = O2 + 2048
O4 = O3 + 1024
N_IN = O4 + 4096
NCORE = 8
NS = 16
TOWN = 1024
C0 = float(np.exp(-0.5))

CV_MU = 0
CV_W0 = 25
CV_A0 = 33
CV_KK = 41
CV_KA = 49
CV_RK = 57
CV_LG = 65
CV_LB = 73
CV_CB = 81
CV_CG = 89
CV_CLB = 97
CV_CW = 105
CV_N = 105 + 248
MK_SC = 0
MK_MU = 128
MK_ML = 192
MK_RM = 256
MK_BO = 768
MK_ON = 896
MK_N = 1024


class Reg:
    __slots__ = ("name", "w", "rs")

    def __init__(self, name):
        self.name = name
        self.w = None
        self.rs = []


class Sched:
    def __init__(self, nc, n_dma_sems=6):
        self.nc = nc
        self.engs = {}
        self.ops = {}
        self.sems = {}
        self.known = {}
        self.count = {}
        for name in ("tensor", "vector", "scalar", "gpsimd", "sync"):
            self.engs[name] = getattr(nc, name)
            self.ops[name] = []
            self.known[name] = {}
        for name in ("tensor", "vector", "scalar", "gpsimd"):
            self._mksem("e_" + name)
        self.dma_sems = {}
        self.dma_rr = {}
        for q in ("sync", "gpsimd", "scalar"):
            self.dma_sems[q] = []
            for i in range(n_dma_sems):
                k = "d_%s%d" % (q, i)
                self._mksem(k)
                self.dma_sems[q].append(k)
            self.dma_rr[q] = 0
        self.nwaits = 0
        self.nops = 0
        self.epoch = {}

    def _mksem(self, key):
        self.sems[key] = self.nc.alloc_semaphore(key)
        self.count[key] = 0

    def _deps(self, reads, writes):
        deps = {}
        for r in reads:
            if r.w is not None:
                k, v = r.w
                deps[k] = max(deps.get(k, 0), v)
        for r in writes:
            if r.w is not None:
                k, v = r.w
                deps[k] = max(deps.get(k, 0), v)
            for (k, v) in r.rs:
                deps[k] = max(deps.get(k, 0), v)
        return deps

    def _emit_waits(self, eng, deps):
        kn = self.known[eng]
        e = self.engs[eng]
        for k, v in deps.items():
            if kn.get(k, 0) >= v:
                continue
            kn[k] = v
            sem = self.sems[k]
            self.ops[eng].append(lambda e=e, sem=sem, v=v: e.wait_ge(sem, v))
            self.nwaits += 1

    def _record(self, ev, reads, writes):
        for r in reads:
            r.rs.append(ev)
            if len(r.rs) > 64:
                m = {}
                for (k, v) in r.rs:
                    m[k] = max(m.get(k, 0), v)
                r.rs = list(m.items())
        for r in writes:
            r.w = ev
            r.rs = []

    SEM_LIMIT = 3500

    def _cur_key(self, basekey):
        ep = self.epoch.get(basekey, 0)
        key = basekey if ep == 0 else "%s#%d" % (basekey, ep)
        if self.count[key] >= self.SEM_LIMIT:
            ep += 1
            self.epoch[basekey] = ep
            key = "%s#%d" % (basekey, ep)
            self._mksem(key)
        return key

    def op(self, eng, fn, reads=(), writes=(), sync_same=True):
        key = self._cur_key("e_" + eng)
        deps = self._deps(reads, writes)
        if not sync_same:
            for k in [k for k in deps if k.split("#")[0] == "e_" + eng]:
                deps.pop(k, None)
        self._emit_waits(eng, deps)
        self.count[key] += 1
        v = self.count[key]
        sem = self.sems[key]
        e = self.engs[eng]
        self.ops[eng].append(lambda: fn(e).then_inc(sem, 1))
        self.nops += 1
        ev = (key, v)
        self._record(ev, reads, writes)
        return ev

    def op_group(self, eng, items):
        base = "e_" + eng
        deps = {}
        for (fn, reads, writes) in items:
            for k, v in self._deps(reads, writes).items():
                deps[k] = max(deps.get(k, 0), v)
        for k in [k for k in deps if k.split("#")[0] == base]:
            deps.pop(k, None)
        self._emit_waits(eng, deps)
        for (fn, reads, writes) in items:
            self.op(eng, fn, reads, writes, sync_same=False)

    def dma(self, q, out, in_, reads=(), writes=(), **kw):
        i = self.dma_rr[q]
        self.dma_rr[q] = (i + 1) % len(self.dma_sems[q])
        key = self._cur_key(self.dma_sems[q][i])
        deps = self._deps(reads, writes)
        if self.count[key] > 0:
            deps[key] = max(deps.get(key, 0), self.count[key])
        self._emit_waits(q, deps)
        self.count[key] += 16
        v = self.count[key]
        sem = self.sems[key]
        e = self.engs[q]
        self.ops[q].append(lambda: e.dma_start(out=out, in_=in_, **kw).then_inc(sem, 16))
        self.nops += 1
        ev = (key, v)
        self._record(ev, reads, writes)
        return ev

    def barrier(self):
        for eng in ("tensor", "vector", "scalar", "gpsimd", "sync"):
            deps = {k: v for k, v in self.count.items() if v > 0}
            self._emit_waits(eng, deps)

    def emit(self):
        nc = self.nc
        with nc.Block() as block:
            @block.sync
            def _(e):
                for f in self.ops["sync"]:
                    f()

            @block.tensor
            def _(e):
                for f in self.ops["tensor"]:
                    f()

            @block.vector
            def _(e):
                for f in self.ops["vector"]:
                    f()

            @block.scalar
            def _(e):
                for f in self.ops["scalar"]:
                    f()

            @block.gpsimd
            def _(e):
                for f in self.ops["gpsimd"]:
                    f()


def build(stage=99):
    POOL = os.environ.get('KPOOL', 'vector')
    SUB = int(os.environ.get('KSUB', '9'))
    nc = bass.Bass("TRN2", target_bir_lowering=False)
    S = Sched(nc)

    def din(name, shape, dt=F32):
        return nc.dram_tensor(name, list(shape), dt, kind="ExternalInput").ap()

    def dout(name, shape, dt=F32):
        return nc.dram_tensor(name, list(shape), dt, kind="ExternalOutput").ap()

    xs_d = din("xs", [2048 + NS, D])
    w_in_d = din("w_in", [D, N_IN])
    w_pa_d = din("w_pa", [DA, D])
    w_pb_d = din("w_pb", [DA, D])
    w_out_d = din("w_out", [D, D])
    w_pg_d = din("w_pg", [D, D])
    w_ple_d = din("w_ple", [256, D])
    ng_d = din("ng", [1, D])
    fg_d = din("fg", [1, D])
    fgT_d = din("fgT", [128, 16])
    cvec_d = din("cvec", [128, CV_N])
    lwp_d = din("lwp", [128, 2, DA])
    ssT_d = din("ssT", [128, 25, NS])
    wkvT_d = din("wkvT", [NS, 8, 128, 64])
    scT_d = din("scT", [128, 8, NS, 30])
    pp_d = din("pp", [TOWN + NS, 256])
    identb_d = din("identb", [128, 128], BF16)
    identf_d = din("identf", [128, 128])
    msk_d = din("msk", [128, MK_N])

    y_d = dout("y", [TOWN + NS, D])
    nsp_d = dout("nsp", [128, 25])
    nwp_d = dout("nwp", [128, 8, 64])
    ncp_d = dout("ncp", [128, 8, 30])
    nss_d = dout("nss", [128, 25, NS])
    nws_d = dout("nws", [NS, 8, 128, 64])
    ncs_d = dout("ncs", [128, 8, NS, 30])
    out_regs = {k: Reg("o_" + k) for k in ["y", "nsp", "nwp", "ncp", "nss", "nws", "ncs"]}

    ARENA = 205440
    slab = nc.alloc_sbuf_tensor("slab", [128, ARENA // 4], F32)
    base = nc.lookup_mloc(slab).addr
    cur = {"P": 0, "U": 0}
    PSZ = 102 * 1024
    names = [0]

    def sb(name, shape, dt=F32, reg="P"):
        nbytes = int(np.prod(shape[1:])) * (2 if dt == BF16 else 4)
        nbytes = (nbytes + 63) // 64 * 64
        off = cur[reg]
        cur[reg] += nbytes
        if reg == "P":
            assert cur[reg] <= PSZ, (name, cur[reg])
            a = base + off
        else:
            assert PSZ + cur[reg] <= ARENA, (name, cur[reg])
            a = base + PSZ + off
        names[0] += 1
        return nc.alloc_sbuf_tensor_at("t%d_%s" % (names[0], name), list(shape), dt, offset=a)

    def reset_union():
        S.barrier()
        cur["U"] = 0

    cvec = sb("cvec", [128, CV_N]); r_cvec = Reg("cvec")
    identb = sb("identb", [128, 128], BF16); r_identb = Reg("identb")
    identf = sb("identf", [128, 128]); r_identf = Reg("identf")
    msk = sb("msk", [128, MK_N]); r_msk = Reg("msk")
    lwp = sb("lwp", [128, 2, DA]); r_lwp = Reg("lwp")
    omka = sb("omka", [128, 8]); r_omka = Reg("omka")
    S.dma("sync", cvec[:], cvec_d, writes=[r_cvec])
    S.dma("sync", identb[:], identb_d, writes=[r_identb])
    S.dma("sync", identf[:], identf_d, writes=[r_identf])
    S.dma("sync", msk[:], msk_d, writes=[r_msk])
    S.dma("sync", lwp[:], lwp_d, writes=[r_lwp])
    S.op("vector", lambda e: e.tensor_scalar(out=omka[:], in0=cvec[:, CV_KA:CV_KA + 8], scalar1=-1.0, scalar2=1.0,
                                             op0=ALU.mult, op1=ALU.add), reads=[r_cvec], writes=[r_omka])
    MSC = msk[:, MK_SC:MK_SC + 128]
    MU = msk[0:64, MK_MU:MK_MU + 64]
    ML = msk[0:64, MK_ML:MK_ML + 64]
    RMASK = msk[:, MK_RM:MK_RM + 512]
    BONES = msk[:, MK_BO:MK_BO + 128]
    ONES = msk[:, MK_ON:MK_ON + 128]
    IDF64 = identf[0:64, 0:64]

    pg = [nc.alloc_psum_tensor("pg%d" % i, [128, 512], F32) for i in range(2)]
    r_pg = [Reg("pg%d" % i) for i in range(2)]
    pgs = nc.alloc_psum_tensor("pgs", [128, 512], F32); r_pgs = Reg("pgs")
    ptb = nc.alloc_psum_tensor("ptb", [128, 1024], BF16); r_ptb = Reg("ptb")
    pq = [nc.alloc_psum_tensor("pq%d" % i, [128, 512], F32) for i in range(4)]
    r_pq = [Reg("pq%d" % i) for i in range(4)]

    XW = TOWN + NS
    xnT = sb("xnT", [128, 16, XW], BF16); r_xnT = Reg("xnT")
    xhist = sb("xhist", [128, 16, 32], BF16); r_xhist = Reg("xhist")
    NSLOT = 6
    wF = sb("wF", [128, NSLOT, 16, 128], BF16)
    r_wF = [Reg("wF%d" % i) for i in range(NSLOT)]
    ogT = sb("ogT", [128, 8, XW], BF16); r_og = [Reg("og%d" % g) for g in range(8)]
    carry = sb("carry", [128, 25]); r_carry = Reg("carry")
    ssT = sb("ssT", [128, 25, NS]); r_ssT = Reg("ssT")
    nss_sb = sb("nss_sb", [128, 25, NS]); r_nss = Reg("nss_sb")
    Hsave = sb("Hsave", [128, 8, 128]); r_Hsave = [Reg("Hsave%d" % g) for g in range(8)]
    nwp_sb = sb("nwp_sb", [128, 8, 64]); r_nwp = Reg("nwp_sb")
    ssq = sb("ssq", [128, 2]); r_ssq = [Reg("ssq0"), Reg("ssq1")]
    S.dma("sync", ssT[:], ssT_d, writes=[r_ssT])
    S.op(POOL, lambda e: e.memset(carry[:], 0.0), writes=[r_carry])
    S.op(POOL, lambda e: e.memset(Hsave[:], 0.0), writes=r_Hsave)

    chunks = []

    def add_chunk(src, col, K=D):
        chunks.append((src[:, col:col + 128], K // 128))
        return len(chunks) - 1

    issued = [0]
    released = set()
    low = [0]

    def wpump():
        while low[0] in released:
            low[0] += 1
        while issued[0] < len(chunks) and issued[0] < low[0] + NSLOT:
            j = issued[0]
            src, kc = chunks[j]
            sl = j % NSLOT
            if not (os.environ.get("KNOW") == "1" and j >= NSLOT):
                S.dma("gpsimd", wF[:, sl, 0:kc, :], src.rearrange("(c p) n -> p c n", p=128), writes=[r_wF[sl]])
            issued[0] += 1

    def wdone(ci):
        released.add(ci)
        wpump()

    def wget(ci):
        wpump()
        assert issued[0] > ci, (ci, issued[0], low[0])
        return ci % NSLOT, chunks[ci][1]

    def gemmF(ci, rhs_t, r_rhs, toks):
        sl, kc = wget(ci)
        for (c0, n, pap, preg) in toks:
            for k in range(kc):
                S.op("tensor", lambda e, sl=sl, k=k, c0=c0, n=n, pap=pap, kc=kc: e.matmul(
                    pap, lhsT=wF[:, sl, k, :], rhs=rhs_t[:, k, c0:c0 + n], start=(k == 0), stop=(k == kc - 1)),
                     reads=[r_wF[sl], r_rhs], writes=[preg], sync_same=False)

    plan = {}
    for ps in ("P", "O"):
        plan[(ps, "xwxa")] = add_chunk(w_in_d, 3072)
        for g in range(8):
            for nm, b0 in (("r", 0), ("k", 1024), ("v", 2048)):
                plan[(ps, nm, g)] = add_chunk(w_in_d, b0 + 128 * g)
            if ps == "O":
                plan[("ga", g)] = add_chunk(w_in_d, O1 + 128 * g)
    for c in range(8):
        plan[("glua", c)] = add_chunk(w_in_d, O2 + 128 * c)
        plan[("glub", c)] = add_chunk(w_in_d, O2 + 1024 + 128 * c)
    for c in range(8):
        plan[("gb", c)] = add_chunk(w_in_d, O3 + 128 * c)
    for dc in range(16):
        plan[("pa", dc)] = add_chunk(w_pa_d, 128 * dc, K=DA)
        plan[("mga", dc)] = add_chunk(w_in_d, O4 + 128 * dc)
        plan[("pb", dc)] = add_chunk(w_pb_d, 128 * dc, K=DA)
        plan[("mgb", dc)] = add_chunk(w_in_d, O4 + 2048 + 128 * dc)
    for hh in range(2):
        for ec in range(16):
            plan[("wo", ec, hh)] = add_chunk(w_out_d, 128 * ec)
        for fc in range(16):
            plan[("pgt", fc, hh)] = add_chunk(w_pg_d, 128 * fc)
            plan[("ple", fc, hh)] = add_chunk(w_ple_d, 128 * fc, K=256)

    globals_cache = {}

    def run_pass(ps):
        has_s = (ps == "O")
        reset_union()
        gbc = sb("gbc", [128, D], reg="U"); r_gbc = Reg("gbc")
        xt = [sb("xt%d" % i, [128, D], reg="U") for i in range(2)]
        r_xt = [Reg("xt%d" % i) for i in range(2)]
        xnb = [sb("xnb%d" % i, [128, D], BF16, reg="U") for i in range(2)]
        r_xnb = [Reg("xnb%d" % i) for i in range(2)]
        junk = sb("junk", [128, D], BF16, reg="U"); r_junk = Reg("junk")
        S.dma("sync", gbc[:], ng_d.partition_broadcast(128), writes=[r_gbc])
        tilecnt = [0]

        def build_xnT(row0, nrows, col0):
            t0 = 0
            while t0 < nrows:
                n = min(128, nrows - t0)
                i = tilecnt[0] % 2
                tilecnt[0] += 1
                S.dma("sync", xt[i][0:n, :], xs_d[row0 + t0:row0 + t0 + n, :], writes=[r_xt[i]])
                S.op("scalar", lambda e, i=i, n=n: e.activation(out=junk[0:n, :], in_=xt[i][0:n, :], func=AF.Square,
                                                                accum_out=ssq[0:n, i:i + 1]),
                     reads=[r_xt[i]], writes=[r_junk, r_ssq[i]])
                S.op("scalar", lambda e, i=i, n=n: e.activation(out=ssq[0:n, i:i + 1], in_=ssq[0:n, i:i + 1], func=AF.Sqrt,
                                                                scale=1.0 / D, bias=1e-6),
                     reads=[r_ssq[i]], writes=[r_ssq[i]])
                S.op("vector", lambda e, i=i, n=n: e.reciprocal(out=ssq[0:n, i:i + 1], in_=ssq[0:n, i:i + 1]),
                     reads=[r_ssq[i]], writes=[r_ssq[i]])
                S.op("vector", lambda e, i=i, n=n: e.scalar_tensor_tensor(out=xnb[i][0:n, :], in0=xt[i][0:n, :],
                                                                          scalar=ssq[0:n, i:i + 1], in1=gbc[0:n, :],
                                                                          op0=ALU.mult, op1=ALU.mult),
                     reads=[r_xt[i], r_ssq[i], r_gbc], writes=[r_xnb[i]])
                for half in range(2):
                    for c in range(8):
                        cc = half * 8 + c
                        S.op("tensor", lambda e, i=i, n=n, c=c, cc=cc: e.transpose(
                            ptb[:, c * 128:c * 128 + n], xnb[i][0:n, cc * 128:(cc + 1) * 128], identb[0:n, 0:n]),
                             reads=[r_xnb[i], r_identb], writes=[r_ptb], sync_same=False)
                    S.op("scalar", lambda e, half=half, n=n, t0=t0: e.copy(
                        out=xnT[:, half * 8:(half + 1) * 8, col0 + t0:col0 + t0 + n],
                        in_=ptb[:].rearrange("p (c t) -> p c t", c=8)[:, :, 0:n]),
                         reads=[r_ptb], writes=[r_xnT])
                t0 += n

        if ps == "P":
            build_xnT(0, 1024, 0)
            S.op("vector", lambda e: e.tensor_copy(out=xhist[:], in_=xnT[:, :, 992:1024]), reads=[r_xnT], writes=[r_xhist])
        else:
            build_xnT(1024, 1024 + NS, 0)
        reset_union()

        def ub(name, shape, dt=F32):
            return sb(name, shape, dt, reg="U"), Reg(name)

        praw = {}
        r_praw = {}
        _pr, _rpr = ub("praw", [128, 513])
        for nm in ("r", "k", "v", "x"):
            praw[nm], r_praw[nm] = _pr, _rpr
        Hin, r_Hin = ub("Hin", [128, 8, 64])
        Hout, r_Hout = ub("Hout", [128, 8, 64])
        tmpd, r_tmpd = ub("tmpd", [128, 512])
        stmp, r_stmp = ub("stmp", [128, NS])
        XR, r_XR = ub("XR", [128, 512]); XK, r_XK = ub("XK", [128, 512]); XV, r_XV = ub("XV", [128, 512])
        XRs, r_XRs = ub("XRs", [128, NS]); XKs, r_XKs = ub("XKs", [128, NS]); XVs, r_XVs = ub("XVs", [128, NS])
        LWIN, r_LWIN = ub("LWIN", [128, TOWN + NS])
        LD, r_LD = ub("LD", [128, 512]); AA, r_AA = ub("AA", [128, 512])
        LDs, r_LDs = ub("LDs", [128, NS]); AAs, r_AAs = ub("AAs", [128, NS])
        KK, r_KK = ub("KK", [128, 512]); T1, r_T1 = ub("T1", [128, 512]); T2, r_T2 = ub("T2", [128, 512])
        KF, r_KF = ub("KF", [128, 512]); BV, r_BV = ub("BV", [128, 512]); BON, r_BON = ub("BON", [128, 512])
        CUM, r_CUM = ub("CUM", [128, 512]); EP, r_EP = ub("EP", [128, 512]); EM, r_EM = ub("EM", [128, 512])
        AR, r_AR = ub("AR", [128, 8, 2, 64])
        BKp = []; r_BKp = []
        SC = []; r_SC = []
        BKtp = []; r_BKtp = []
        UVp = []; r_UVp = []
        for hd in range(2):
            t, r = ub("BKp%d" % hd, [128, 8, 2, 64]); BKp.append(t); r_BKp.append(r)
            t, r = ub("SC%d" % hd, [128, 8, 2, 64]); SC.append(t); r_SC.append(r)
            t, r = ub("BKtp%d" % hd, [128, 8, 128]); BKtp.append(t); r_BKtp.append(r)
            t, r = ub("UVp%d" % hd, [128, 8, 128]); UVp.append(t); r_UVp.append(r)
            S.op(POOL, lambda e, t=BKp[hd]: e.memset(t[:], 0.0), writes=[r_BKp[hd]])
            S.op(POOL, lambda e, t=UVp[hd]: e.memset(t[:], 0.0), writes=[r_UVp[hd]])
        Ttp, r_Ttp = ub("Ttp", [128, 8, 2, 128])
        S.op(POOL, lambda e: e.memset(Ttp[:], 0.0), writes=[r_Ttp])
        GC, r_GC = ub("GC", [128, 8])
        PSb, r_PSb = ub("PSb", [128, 8, 2, 64], BF16)
        Qb, r_Qb = ub("Qb", [128, 8, 64], BF16)
        S.op(POOL, lambda e: e.memset(PSb[:], 0.0), writes=[r_PSb])
        S.op(POOL, lambda e: e.memset(Qb[:], 0.0), writes=[r_Qb])
        SfA, r_SfA = EP[0:64, :].rearrange("p (c t) -> p c t", t=64), r_EP
        SfB, r_SfB = CUM[0:64, :].rearrange("p (c t) -> p c t", t=64), r_CUM
        Pm, r_Pm = EM[0:64, :].rearrange("p (c t) -> p c t", t=64), r_EM
        Hbd = []; r_Hbd = []
        for i in range(2):
            if ("Hbd%d" % i) not in globals_cache:
                globals_cache["Hbd%d" % i] = sb("Hbdp%d" % i, [128, 128])
            t, r = globals_cache["Hbd%d" % i], Reg("Hbd%d" % i)
            Hbd.append(t); r_Hbd.append(r)
            S.op(POOL, lambda e, t=t: e.memset(t[:], 0.0), writes=[r])
        if "HGp" not in globals_cache:
            globals_cache["HGp"] = sb("HGp", [128, 128])
        HG, r_HG = globals_cache["HGp"], Reg("HG")
        Zs, r_Zs = ub("Zs", [128, 128])
        S.op(POOL, lambda e: e.memset(Zs[:], 0.0), writes=[r_Zs])
        YT, r_YT = BV, r_BV
        YTs, r_YTs = ub("YTs", [128, NS]); BONs, r_BONs = ub("BONs", [128, NS])
        Up = []; r_Up = []
        for hd in range(2):
            t, r = ub("Up%d" % hd, [128, 8, 128]); Up.append(t); r_Up.append(r)
            S.op(POOL, lambda e, t=t: e.memset(t[:], 0.0), writes=[r])
        G1, r_G1 = KK, r_KK
        G2, r_G2 = T1, r_T1
        G3, r_G3 = T2, r_T2

        def shift_evac(nm, ci25, pap, preg, n, dst, r_dst):
            pr = praw[nm]
            rp = r_praw[nm]
            S.op("scalar", lambda e: e.copy(out=pr[:, 0:1], in_=carry[:, ci25:ci25 + 1]), reads=[r_carry], writes=[rp])
            S.op("scalar", lambda e: e.copy(out=pr[:, 1:1 + n], in_=pap), reads=[preg], writes=[rp])
            S.op("scalar", lambda e: e.copy(out=carry[:, ci25:ci25 + 1], in_=pr[:, n:n + 1]), reads=[rp], writes=[r_carry])
            S.op("vector", lambda e: e.tensor_sub(out=tmpd[:, 0:n], in0=pr[:, 0:n], in1=pr[:, 1:1 + n]),
                 reads=[rp], writes=[r_tmpd])
            S.op("vector", lambda e: e.scalar_tensor_tensor(out=dst, in0=tmpd[:, 0:n], scalar=cvec[:, CV_MU + ci25:CV_MU + ci25 + 1],
                                                            in1=pr[:, 1:1 + n], op0=ALU.mult, op1=ALU.add),
                 reads=[r_tmpd, rp, r_cvec], writes=[r_dst])

        def shift_evac_sample(ci25, pap, preg, dst, r_dst):
            S.op("scalar", lambda e: e.copy(out=nss_sb[:, ci25, :], in_=pap), reads=[preg], writes=[r_nss])
            S.op("vector", lambda e: e.tensor_sub(out=stmp[:], in0=ssT[:, ci25, :], in1=nss_sb[:, ci25, :]),
                 reads=[r_ssT, r_nss], writes=[r_stmp])
            S.op("vector", lambda e: e.scalar_tensor_tensor(out=dst, in0=stmp[:], scalar=cvec[:, CV_MU + ci25:CV_MU + ci25 + 1],
                                                            in1=nss_sb[:, ci25, :], op0=ALU.mult, op1=ALU.add),
                 reads=[r_stmp, r_nss, r_cvec], writes=[r_dst])

        def fl(ap):
            return ap.rearrange("p w t -> p (w t)")

        def c3(t):
            return t[:].rearrange("p (c t) -> p c t", t=64)

        def prep(g, lw_cols, sample_tile, want_y):
            cw0 = cvec[:, CV_W0 + g:CV_W0 + g + 1]
            ca0 = cvec[:, CV_A0 + g:CV_A0 + g + 1]
            ckk = cvec[:, CV_KK + g:CV_KK + g + 1]
            cka = cvec[:, CV_KA + g:CV_KA + g + 1]
            crk = cvec[:, CV_RK + g:CV_RK + g + 1]
            if not sample_tile:
                S.op("tensor", lambda e: e.matmul(pq[0][:, :], lhsT=lwp[:, 0, g * 128:(g + 1) * 128], rhs=LWIN[:, lw_cols:lw_cols + 512],
                                                  start=True, stop=True), reads=[r_lwp, r_LWIN], writes=[r_pq[0]], sync_same=False)
                S.op("tensor", lambda e: e.matmul(pq[1][:, :], lhsT=lwp[:, 1, g * 128:(g + 1) * 128], rhs=LWIN[:, lw_cols:lw_cols + 512],
                                                  start=True, stop=True), reads=[r_lwp, r_LWIN], writes=[r_pq[1]], sync_same=False)
                S.op("scalar", lambda e: e.activation(out=LD[:], in_=pq[0][:, :], func=AF.Sigmoid, bias=cw0),
                     reads=[r_pq[0], r_cvec], writes=[r_LD])
                S.op("scalar", lambda e: e.activation(out=AA[:], in_=pq[1][:, :], func=AF.Sigmoid, bias=ca0),
                     reads=[r_pq[1], r_cvec], writes=[r_AA])
            S.op(POOL, lambda e: e.tensor_scalar(out=KK[:], in0=XK[:], scalar1=ckk, scalar2=None, op0=ALU.mult),
                 reads=[r_XK, r_cvec], writes=[r_KK])
            S.op(POOL, lambda e: e.tensor_mul(out=T1[:], in0=KK[:], in1=KK[:]), reads=[r_KK], writes=[r_T1])
            S.op("tensor", lambda e: e.matmul(pq[2][:, :], lhsT=BONES, rhs=T1[:], start=True, stop=True),
                 reads=[r_msk, r_T1], writes=[r_pq[2]], sync_same=False)
            S.op("scalar", lambda e: e.activation(out=T2[:], in_=pq[2][:, :], func=AF.Sqrt), reads=[r_pq[2]], writes=[r_T2])
            S.op("vector", lambda e: e.tensor_scalar_max(out=T2[:], in0=T2[:], scalar1=1e-12), reads=[r_T2], writes=[r_T2])
            S.op("vector", lambda e: e.reciprocal(out=T2[:], in_=T2[:]), reads=[r_T2], writes=[r_T2])
            S.op("vector", lambda e: e.tensor_mul(out=KK[:], in0=KK[:], in1=T2[:]), reads=[r_KK, r_T2], writes=[r_KK])
            S.op("vector", lambda e: e.tensor_scalar(out=T1[:], in0=AA[:], scalar1=cka, scalar2=omka[:, g:g + 1],
                                                     op0=ALU.mult, op1=ALU.add), reads=[r_AA, r_cvec, r_omka], writes=[r_T1])
            S.op("vector", lambda e: e.tensor_mul(out=KF[:], in0=XK[:], in1=T1[:]), reads=[r_XK, r_T1], writes=[r_KF])
            S.op(POOL, lambda e: e.tensor_mul(out=BV[:], in0=KK[:], in1=AA[:]), reads=[r_KK, r_AA], writes=[r_BV])
            if want_y:
                S.op("vector", lambda e: e.scalar_tensor_tensor(out=T2[:], in0=XR[:], scalar=crk, in1=KF[:],
                                                                op0=ALU.mult, op1=ALU.mult),
                     reads=[r_XR, r_KF, r_cvec], writes=[r_T2])
                S.op("tensor", lambda e: e.matmul(pq[3][:, :], lhsT=BONES, rhs=T2[:], start=True, stop=True),
                     reads=[r_msk, r_T2], writes=[r_pq[3]], sync_same=False)
                S.op("vector", lambda e: e.tensor_mul(out=BON[:], in0=pq[3][:, :], in1=XV[:]),
                     reads=[r_pq[3], r_XV], writes=[r_BON])
            S.op("vector", lambda e: e.tensor_tensor_scan(out=CUM[:], data0=RMASK, data1=LD[:], initial=0.0,
                                                          op0=ALU.mult, op1=ALU.add),
                 reads=[r_msk, r_LD], writes=[r_CUM])
            S.op("scalar", lambda e: e.activation(out=EP[:], in_=CUM[:], func=AF.Exp, scale=-C0), reads=[r_CUM], writes=[r_EP])
            S.op("scalar", lambda e: e.activation(out=EM[:], in_=CUM[:], func=AF.Exp, scale=C0), reads=[r_CUM], writes=[r_EM])
            S.op("vector", lambda e: e.tensor_sub(out=T1[:], in0=CUM[:], in1=LD[:]), reads=[r_CUM, r_LD], writes=[r_T1])
            S.op("scalar", lambda e: e.activation(out=T1[:], in_=T1[:], func=AF.Exp, scale=-C0), reads=[r_T1], writes=[r_T1])
            S.op("vector", lambda e: e.scalar_tensor_tensor(out=AR[:, :, 0, :], in0=c3(KK), scalar=-1.0, in1=c3(T1),
                                                            op0=ALU.mult, op1=ALU.mult),
                 reads=[r_KK, r_T1], writes=[r_AR])
            S.op(POOL, lambda e: e.tensor_mul(out=AR[:, :, 1, :], in0=c3(XR), in1=c3(EP)), reads=[r_XR, r_EP], writes=[r_AR])
            for hd in range(2):
                rs = slice(hd * 64, (hd + 1) * 64)
                S.op("vector", lambda e, hd=hd, rs=rs: e.tensor_mul(out=BKp[hd][rs, :, 0, :], in0=c3(KF)[rs], in1=c3(EM)[rs]),
                     reads=[r_KF, r_EM], writes=[r_BKp[hd]])
                S.op(POOL, lambda e, hd=hd, rs=rs: e.tensor_mul(out=BKp[hd][rs, :, 1, :], in0=c3(BV)[rs], in1=c3(EM)[rs]),
                     reads=[r_BV, r_EM], writes=[r_BKp[hd]])
            S.op("scalar", lambda e: e.copy(out=GC[:], in_=c3(EP)[:, :, 63]), reads=[r_EP], writes=[r_GC])
            for c in range(8):
                b = pq[c // 4]
                S.op("tensor", lambda e, c=c, b=b: e.transpose(b[0:64, (c % 4) * 128:(c % 4 + 1) * 128], XV[:, c * 64:(c + 1) * 64], identf[:, :]),
                     reads=[r_XV, r_identf], writes=[r_pq[c // 4]], sync_same=False)
            for half in range(2):
                src = pq[half][0:64, :].rearrange("p (c x) -> p c x", x=128)
                S.op("scalar", lambda e, half=half, src=src: e.copy(out=UVp[0][0:64, half * 4:(half + 1) * 4, 0:64], in_=src[:, :, 0:64]),
                     reads=[r_pq[half]], writes=[r_UVp[0]])
                S.op("scalar", lambda e, half=half, src=src: e.copy(out=UVp[1][0:64, half * 4:(half + 1) * 4, 64:128], in_=src[:, :, 64:128]),
                     reads=[r_pq[half]], writes=[r_UVp[1]])
            for hd in range(2):
                for c in range(8):
                    b = 2 + (c // 4) % 2
                    S.op("tensor", lambda e, hd=hd, c=c, b=b: e.matmul(pq[b][:, (c % 4) * 128:(c % 4 + 1) * 128],
                                                                       lhsT=fl(BKp[hd][:, c, :, :]), rhs=fl(AR[:, c, :, :]), start=True, stop=True),
                         reads=[r_BKp[hd], r_AR], writes=[r_pq[b]], sync_same=False)
                    if c % 4 == 3:
                        h4 = c // 4
                        S.op("vector", lambda e, hd=hd, h4=h4, b=b: e.tensor_mul(
                            out=SC[hd][:, h4 * 4:(h4 + 1) * 4, :, :].rearrange("p c w t -> p c (w t)"),
                            in0=pq[b][:, :].rearrange("p (c x) -> p c x", x=128),
                            in1=MSC.unsqueeze(1).to_broadcast([128, 4, 128])),
                             reads=[r_pq[b], r_msk], writes=[r_SC[hd]])
            for hd in range(2):
                for c in range(8):
                    b = (c // 4) % 2
                    S.op("tensor", lambda e, hd=hd, c=c, b=b: e.transpose(pq[b][:, (c % 4) * 128:(c % 4 + 1) * 128],
                                                                          fl(BKp[hd][:, c, :, :]), identf[:, :]),
                         reads=[r_BKp[hd], r_identf], writes=[r_pq[b]], sync_same=False)
                    if c % 4 == 3:
                        h4 = c // 4
                        S.op("scalar", lambda e, hd=hd, h4=h4, b=b: e.copy(
                            out=BKtp[hd][:, h4 * 4:(h4 + 1) * 4, :], in_=pq[b][:, :].rearrange("p (c x) -> p c x", x=128)),
                             reads=[r_pq[b]], writes=[r_BKtp[hd]])
            if sample_tile:
                for hd in range(2):
                    S.op("vector", lambda e, hd=hd: e.tensor_copy(out=Ttp[0:64, :, hd, 64:128], in_=IDF64.unsqueeze(1).to_broadcast([64, 8, 64])),
                         reads=[r_identf], writes=[r_Ttp])
            for hd in range(0 if sample_tile else 2):
                for c in range(8):
                    S.op("tensor", lambda e, hd=hd, c=c: e.matmul(pq[2][0:64, c * 64:(c + 1) * 64], lhsT=BKp[hd][:, c, 1, :], rhs=AR[:, c, 0, :],
                                                                  start=True, stop=True),
                         reads=[r_BKp[hd], r_AR], writes=[r_pq[2]], sync_same=False)
                for c in range(8):
                    S.op("tensor", lambda e, hd=hd, c=c: e.matmul(pq[3][0:64, c * 64:(c + 1) * 64], lhsT=AR[:, c, 0, :], rhs=BKp[hd][:, c, 1, :],
                                                                  start=True, stop=True),
                         reads=[r_BKp[hd], r_AR], writes=[r_pq[3]], sync_same=False)
                p3 = lambda t: t[0:64, :].rearrange("p (c x) -> p c x", x=64)
                S.op("vector", lambda e: e.tensor_mul(out=Pm[:], in0=p3(pq[2]), in1=MU.unsqueeze(1).to_broadcast([64, 8, 64])),
                     reads=[r_pq[2], r_msk], writes=[r_Pm])
                S.op("vector", lambda e: e.tensor_mul(out=Qb[0:64], in0=p3(pq[3]), in1=ML.unsqueeze(1).to_broadcast([64, 8, 64])),
                     reads=[r_pq[3], r_msk], writes=[r_Qb])
                S.op("scalar", lambda e: e.copy(out=PSb[0:64, :, 0, :], in_=Pm[:]), reads=[r_Pm], writes=[r_PSb])
                S.op("vector", lambda e: e.tensor_add(out=SfA[:], in0=Pm[:], in1=IDF64.unsqueeze(1).to_broadcast([64, 8, 64])),
                     reads=[r_Pm, r_identf], writes=[r_SfA])
                S.op("scalar", lambda e: e.copy(out=PSb[0:64, :, 1, :], in_=SfA[:]), reads=[r_SfA], writes=[r_PSb])
                scur = [SfA, r_SfA, SfB, r_SfB]
                for lvl in range(int(os.environ.get('KNLV', '6'))):
                    last = (lvl == 5)
                    for c in range(8):
                        b = c // 4
                        if lvl == 0:
                            S.op("tensor", lambda e, c=c, b=b: e.matmul(pq[b][0:64, (c % 4) * 128:(c % 4) * 128 + 64], lhsT=Qb[:, c, :], rhs=PSb[:, c, 0, :],
                                                                        start=True, stop=True),
                                 reads=[r_Qb, r_PSb], writes=[r_pq[b]], sync_same=False)
                        elif last:
                            S.op("tensor", lambda e, c=c, b=b: e.matmul(pq[b][0:64, (c % 4) * 128 + 64:(c % 4 + 1) * 128], lhsT=Qb[:, c, :], rhs=PSb[:, c, 1, :],
                                                                        start=True, stop=True),
                                 reads=[r_Qb, r_PSb], writes=[r_pq[b]], sync_same=False)
                        else:
                            S.op("tensor", lambda e, c=c, b=b: e.matmul(pq[b][0:64, (c % 4) * 128:(c % 4 + 1) * 128], lhsT=Qb[:, c, :], rhs=fl(PSb[:, c, :, :]),
                                                                        start=True, stop=True),
                                 reads=[r_Qb, r_PSb], writes=[r_pq[b]], sync_same=False)
                    if not last:
                        for c in range(8):
                            S.op("tensor", lambda e, c=c: e.matmul(pq[2][0:64, c * 64:(c + 1) * 64], lhsT=PSb[:, c, 0, :], rhs=Qb[:, c, :],
                                                                   start=True, stop=True),
                                 reads=[r_Qb, r_PSb], writes=[r_pq[2]], sync_same=False)
                    So, r_So, Sn, r_Sn = scur
                    for half in range(2):
                        v = pq[half][0:64, :].rearrange("p (c w x) -> p c w x", w=2, x=64)
                        cs = slice(half * 4, (half + 1) * 4)
                        if lvl > 0 and os.environ.get('KVAR', '0') != '1':
                            if last:
                                S.op("vector", lambda e, v=v, cs=cs, hd=hd, So=So: e.tensor_add(out=Ttp[0:64, cs, hd, 64:128], in0=v[:, :, 1, :], in1=So[:, cs, :]),
                                     reads=[r_So, r_pq[half]], writes=[r_Ttp])
                            else:
                                S.op("vector", lambda e, v=v, cs=cs, So=So, Sn=Sn: e.tensor_add(out=Sn[:, cs, :], in0=v[:, :, 1, :], in1=So[:, cs, :]),
                                     reads=[r_So, r_pq[half]], writes=[r_Sn])
                        if not last:
                            S.op("vector", lambda e, v=v, cs=cs: e.tensor_copy(out=PSb[0:64, cs, 0, :], in_=v[:, :, 0, :]),
                                 reads=[r_pq[half]], writes=[r_PSb])
                    if not last:
                        if lvl > 0:
                            S.op("scalar", lambda e, Sn=Sn: e.copy(out=PSb[0:64, :, 1, :], in_=Sn[:]), reads=[r_Sn], writes=[r_PSb])
                            scur = [Sn, r_Sn, So, r_So]
                        S.op("vector", lambda e: e.tensor_copy(out=Qb[0:64], in_=p3(pq[2])), reads=[r_pq[2]], writes=[r_Qb])

        def seq(g, hcur, want_y, sample_b0=None):
            for c in range(8):
                if sample_b0 is not None:
                    b = sample_b0 + c
                    for hd in range(2):
                        rs = slice(hd * 64, (hd + 1) * 64)
                        S.op("scalar", lambda e, c=c, hd=hd, rs=rs, hcur=hcur: e.copy(out=Hbd[hcur][rs, hd * 64:(hd + 1) * 64], in_=Hin[rs, c, :]),
                             reads=[r_Hin], writes=[r_Hbd[hcur]])
                Hc, rHc = Hbd[hcur], r_Hbd[hcur]
                Hn, rHn = Hbd[1 - hcur], r_Hbd[1 - hcur]
                items = [(lambda e, c=c, Hc=Hc: e.matmul(pq[0][0:64, 0:128], lhsT=AR[:, c, 0, :], rhs=Hc[:, :], start=True, stop=False),
                          [r_AR, rHc], [r_pq[0]])]
                for hd in range(2):
                    items.append((lambda e, c=c, hd=hd: e.matmul(pq[0][0:64, 0:128], lhsT=SC[hd][:, c, 0, :], rhs=UVp[hd][:, c, :],
                                                                 start=False, stop=(hd == 1)),
                                  [r_SC[hd], r_UVp[hd]], [r_pq[0]]))
                S.op_group("tensor", items)
                S.op("scalar", lambda e: e.copy(out=Zs[0:64, :], in_=pq[0][0:64, 0:128]), reads=[r_pq[0]], writes=[r_Zs])
                if int(os.environ.get('KSEQ', '9')) >= 2:
                    if os.environ.get("KNOHG") != "1":
                        S.op("vector", lambda e, c=c, Hc=Hc: e.tensor_scalar(out=HG[:], in0=(identf[:, :] if os.environ.get('KHGI') == '1' else Hc[:, :]), scalar1=(1.0 if os.environ.get("KGC1") == "1" else GC[:, c:c + 1]), scalar2=None, op0=ALU.mult),
                             reads=[rHc, r_GC], writes=[r_HG])
                    for hd in range(2):
                        S.op("tensor", lambda e, c=c, hd=hd: e.matmul(pq[1][:, hd * 64:(hd + 1) * 64], lhsT=Ttp[:, c, hd, :], rhs=Zs[:, hd * 64:(hd + 1) * 64],
                                                                      start=True, stop=True),
                             reads=[r_Ttp, r_Zs], writes=[r_pq[1]], sync_same=False)
                    S.op("scalar", lambda e, c=c: e.copy(out=Up[0][:, c, 0:64], in_=pq[1][:, 0:64]), reads=[r_pq[1]], writes=[r_Up[0]])
                    S.op("scalar", lambda e, c=c: e.copy(out=Up[1][:, c, 64:128], in_=pq[1][:, 64:128]), reads=[r_pq[1]], writes=[r_Up[1]])
                if int(os.environ.get('KSEQ', '9')) >= 3:
                    items = []
                    for hd in range(2):
                        items.append((lambda e, c=c, hd=hd: e.matmul(pq[2][:, 0:128], lhsT=BKtp[hd][:, c, :], rhs=UVp[hd][:, c, :],
                                                                     start=(hd == 0), stop=False),
                                      [r_BKtp[hd], r_UVp[hd]], [r_pq[2]]))
                        items.append((lambda e, c=c, hd=hd: e.matmul(pq[2][:, 0:128], lhsT=BKtp[hd][:, c, :], rhs=Up[hd][:, c, :],
                                                                     start=False, stop=(hd == 1)),
                                      [r_BKtp[hd], r_Up[hd]], [r_pq[2]]))
                    S.op_group("tensor", items)
                if int(os.environ.get('KSEQ', '9')) >= 5:
                    if want_y:
                        items = [(lambda e, c=c, Hc=Hc: e.matmul(pq[3][:, c * 64:(c + 1) * 64], lhsT=Hc[:, :], rhs=AR[:, c, 1, :], start=True, stop=False),
                                  [r_AR, rHc], [r_pq[3]])]
                        for hd in range(2):
                            items.append((lambda e, c=c, hd=hd: e.matmul(pq[3][:, c * 64:(c + 1) * 64], lhsT=UVp[hd][:, c, :], rhs=SC[hd][:, c, 1, :],
                                                                         start=False, stop=False),
                                          [r_SC[hd], r_UVp[hd]], [r_pq[3]]))
                            items.append((lambda e, c=c, hd=hd: e.matmul(pq[3][:, c * 64:(c + 1) * 64], lhsT=Up[hd][:, c, :], rhs=SC[hd][:, c, 1, :],
                                                                         start=False, stop=(hd == 1)),
                                          [r_SC[hd], r_Up[hd]], [r_pq[3]]))
                        S.op_group("tensor", items)
                if int(os.environ.get('KSEQ', '9')) >= 4:
                    S.op("vector", lambda e, c=c, Hn=Hn: e.scalar_tensor_tensor(out=Hn[:, :], in0=pq[2][:, 0:128], scalar=GC[:, c:c + 1], in1=HG[:],
                                                                                op0=ALU.mult, op1=ALU.add),
                         reads=[r_pq[2], r_GC, r_HG], writes=[rHn])
                if sample_b0 is not None:
                    b = sample_b0 + c
                    for hd in range(2):
                        rs = slice(hd * 64, (hd + 1) * 64)
                        S.op("scalar", lambda e, c=c, hd=hd, rs=rs, Hn=Hn: e.copy(out=Hout[rs, c, :], in_=Hn[rs, hd * 64:(hd + 1) * 64]),
                             reads=[rHn], writes=[r_Hout])
                else:
                    hcur = 1 - hcur
            return hcur

        def gn_gate(g, yt, r_yt, bon, r_bon, n, gap, r_gap, dst):
            clg = cvec[:, CV_LG + g:CV_LG + g + 1]
            clb = cvec[:, CV_LB + g:CV_LB + g + 1]
            S.op("tensor", lambda e: e.matmul(pq[0][:, 0:n], lhsT=BONES, rhs=yt, start=True, stop=True),
                 reads=[r_msk, r_yt], writes=[r_pq[0]], sync_same=False)
            S.op("scalar", lambda e: e.activation(out=G1[:, 0:n], in_=yt, func=AF.Square), reads=[r_yt], writes=[r_G1])
            S.op("tensor", lambda e: e.matmul(pq[1][:, 0:n], lhsT=BONES, rhs=G1[:, 0:n], start=True, stop=True),
                 reads=[r_msk, r_G1], writes=[r_pq[1]], sync_same=False)
            S.op("vector", lambda e: e.tensor_scalar(out=G2[:, 0:n], in0=pq[0][:, 0:n], scalar1=1.0 / 64, scalar2=None, op0=ALU.mult),
                 reads=[r_pq[0]], writes=[r_G2])
            S.op("vector", lambda e: e.tensor_mul(out=G3[:, 0:n], in0=G2[:, 0:n], in1=G2[:, 0:n]), reads=[r_G2], writes=[r_G3])
            S.op("vector", lambda e: e.scalar_tensor_tensor(out=KF[:, 0:n], in0=pq[1][:, 0:n], scalar=1.0 / 64, in1=G3[:, 0:n],
                                                            op0=ALU.mult, op1=ALU.subtract),
                 reads=[r_pq[1], r_G3], writes=[r_KF])
            S.op("scalar", lambda e: e.activation(out=G3[:, 0:n], in_=KF[:, 0:n], func=AF.Sqrt, bias=64e-5), reads=[r_KF], writes=[r_G3])
            S.op("vector", lambda e: e.reciprocal(out=G3[:, 0:n], in_=G3[:, 0:n]), reads=[r_G3], writes=[r_G3])
            S.op("vector", lambda e: e.tensor_sub(out=G1[:, 0:n], in0=yt, in1=G2[:, 0:n]), reads=[r_yt, r_G2], writes=[r_G1])
            S.op("vector", lambda e: e.tensor_mul(out=G1[:, 0:n], in0=G1[:, 0:n], in1=G3[:, 0:n]), reads=[r_G1, r_G3], writes=[r_G1])
            S.op("vector", lambda e: e.tensor_scalar(out=G1[:, 0:n], in0=G1[:, 0:n], scalar1=clg, scalar2=clb, op0=ALU.mult, op1=ALU.add),
                 reads=[r_G1, r_cvec], writes=[r_G1])
            S.op("vector", lambda e: e.tensor_add(out=G1[:, 0:n], in0=G1[:, 0:n], in1=bon), reads=[r_G1, r_bon], writes=[r_G1])
            S.op("scalar", lambda e: e.activation(out=G2[:, 0:n], in_=gap, func=AF.Silu), reads=[r_gap], writes=[r_G2])
            S.op("vector", lambda e: e.tensor_mul(out=dst, in0=G1[:, 0:n], in1=G2[:, 0:n]), reads=[r_G1, r_G2], writes=[r_og[g]])

        ci = plan[(ps, "xwxa")]
        toks = [(0, 512, pg[0][:, :], r_pg[0]), (512, 512, pg[1][:, :], r_pg[1])]
        if has_s:
            toks.append((1024, NS, pgs[:, 0:NS], r_pgs))
        gemmF(ci, xnT, r_xnT, toks)
        wdone(ci)
        for tt in range(2):
            shift_evac("x", 24, pg[tt][:, :], r_pg[tt], 512, LWIN[:, tt * 512:(tt + 1) * 512], r_LWIN)
        if has_s:
            shift_evac_sample(24, pgs[:, 0:NS], r_pgs, LWIN[:, 1024:1024 + NS], r_LWIN)
        nlw = 1024 + NS if has_s else 1024
        S.op("scalar", lambda e: e.activation(out=LWIN[0:64, 0:nlw], in_=LWIN[0:64, 0:nlw], func=AF.Tanh),
             reads=[r_LWIN], writes=[r_LWIN])
        for g in range(8):
            hcur = 0
            S.op("vector", lambda e, g=g: e.tensor_copy(out=Hbd[0][:, :], in_=Hsave[:, g, :]), reads=[r_Hsave[g]], writes=[r_Hbd[0]])
            for tt in range(2):
                for j, (nm, dst, rd) in enumerate((("r", XR, r_XR), ("k", XK, r_XK), ("v", XV, r_XV))):
                    ci = plan[(ps, nm, g)]
                    bi = (tt * 3 + j) % 2
                    gemmF(ci, xnT, r_xnT, [(tt * 512, 512, pg[bi][:, :], r_pg[bi])])
                    ci25 = {"r": 0, "k": 8, "v": 16}[nm] + g
                    shift_evac(nm, ci25, pg[bi][:, :], r_pg[bi], 512, dst[:, :], rd)
                if stage >= 2:
                    prep(g, tt * 512, False, has_s)
                    if SUB >= 3:
                        hcur = seq(g, hcur, has_s)
                if has_s and stage >= 3:
                    S.op("scalar", lambda e: e.copy(out=YT[:], in_=pq[3][:, :]), reads=[r_pq[3]], writes=[r_YT])
                    gemmF(plan[("ga", g)], xnT, r_xnT, [(tt * 512, 512, pg[0][:, :], r_pg[0])])
                    gn_gate(g, YT[:], r_YT, BON[:], r_BON, 512, pg[0][:, :], r_pg[0], ogT[:, g, tt * 512:(tt + 1) * 512])
            if stage >= 2:
                if ps == "P":
                    S.op("vector", lambda e, g=g, hcur=hcur: e.tensor_copy(out=Hsave[:, g, :], in_=Hbd[hcur][:, :]),
                         reads=[r_Hbd[hcur]], writes=[r_Hsave[g]])
                else:
                    for hd in range(2):
                        rs = slice(hd * 64, (hd + 1) * 64)
                        S.op("vector", lambda e, g=g, hcur=hcur, hd=hd, rs=rs: e.tensor_copy(out=nwp_sb[rs, g, :], in_=Hbd[hcur][rs, hd * 64:(hd + 1) * 64]),
                             reads=[r_Hbd[hcur]], writes=[r_nwp])
            if has_s:
                for j, (nm, dst, rd) in enumerate((("r", XRs, r_XRs), ("k", XKs, r_XKs), ("v", XVs, r_XVs))):
                    ci = plan[(ps, nm, g)]
                    gemmF(ci, xnT, r_xnT, [(1024, NS, pgs[:, j * NS:(j + 1) * NS], r_pgs)])
                    ci25 = {"r": 0, "k": 8, "v": 16}[nm] + g
                    shift_evac_sample(ci25, pgs[:, j * NS:(j + 1) * NS], r_pgs, dst[:, :], rd)
                if stage >= 2 and SUB >= 4:
                    S.op("tensor", lambda e, g=g: e.matmul(pq[0][:, 0:NS], lhsT=lwp[:, 0, g * 128:(g + 1) * 128], rhs=LWIN[:, 1024:1024 + NS],
                                                           start=True, stop=True), reads=[r_lwp, r_LWIN], writes=[r_pq[0]], sync_same=False)
                    S.op("tensor", lambda e, g=g: e.matmul(pq[1][:, 0:NS], lhsT=lwp[:, 1, g * 128:(g + 1) * 128], rhs=LWIN[:, 1024:1024 + NS],
                                                           start=True, stop=True), reads=[r_lwp, r_LWIN], writes=[r_pq[1]], sync_same=False)
                    S.op("scalar", lambda e, g=g: e.activation(out=LDs[:], in_=pq[0][:, 0:NS], func=AF.Sigmoid, bias=cvec[:, CV_W0 + g:CV_W0 + g + 1]),
                         reads=[r_pq[0], r_cvec], writes=[r_LDs])
                    S.op("scalar", lambda e, g=g: e.activation(out=AAs[:], in_=pq[1][:, 0:NS], func=AF.Sigmoid, bias=cvec[:, CV_A0 + g:CV_A0 + g + 1]),
                         reads=[r_pq[1], r_cvec], writes=[r_AAs])
                    for st in range(2):
                        S.dma("sync", Hin[:], wkvT_d[st * 8:(st + 1) * 8, g, :, :].rearrange("b p i -> p b i"), writes=[r_Hin])
                        for (tile, rt, src, rsrc) in ((XR, r_XR, XRs, r_XRs), (XK, r_XK, XKs, r_XKs), (XV, r_XV, XVs, r_XVs),
                                                      (LD, r_LD, LDs, r_LDs), (AA, r_AA, AAs, r_AAs)):
                            S.op(POOL, lambda e, tile=tile: e.memset(tile[:], 0.0), writes=[rt])
                            S.op(POOL, lambda e, tile=tile, src=src, st=st: e.tensor_copy(out=c3(tile)[:, :, 0], in_=src[:, st * 8:(st + 1) * 8]),
                                 reads=[rsrc], writes=[rt])
                        prep(g, 0, True, True)
                        if os.environ.get('KSAMP') != '1':
                            seq(g, 0, True, sample_b0=st * 8)
                            S.dma("sync", nws_d[st * 8:(st + 1) * 8, g, :, :].rearrange("b p i -> p b i"), Hout[:], reads=[r_Hout], writes=[out_regs["nws"]])
                        if stage >= 3:
                            S.op("scalar", lambda e, st=st: e.copy(out=YTs[:, st * 8:(st + 1) * 8], in_=c3(pq[3])[:, :, 0]), reads=[r_pq[3]], writes=[r_YTs])
                            S.op("vector", lambda e, st=st: e.tensor_copy(out=BONs[:, st * 8:(st + 1) * 8], in_=c3(BON)[:, :, 0]), reads=[r_BON], writes=[r_BONs])
                    if stage >= 3:
                        gemmF(plan[("ga", g)], xnT, r_xnT, [(1024, NS, pgs[:, 64:64 + NS], r_pgs)])
                        gn_gate(g, YTs[:], r_YTs, BONs[:], r_BONs, NS, pgs[:, 64:64 + NS], r_pgs, ogT[:, g, 1024:1024 + NS])
            for nm in ("r", "k", "v"):
                wdone(plan[(ps, nm, g)])
            if has_s:
                wdone(plan[("ga", g)])

    run_pass("P")
    run_pass("O")
    S.dma("sync", nsp_d, carry[:], reads=[r_carry], writes=[out_regs["nsp"]])
    S.dma("sync", nss_d, nss_sb[:], reads=[r_nss], writes=[out_regs["nss"]])
    S.dma("sync", nwp_d, nwp_sb[:], reads=[r_nwp], writes=[out_regs["nwp"]])

    reset_union()

    def ub(name, shape, dt=F32):
        return sb(name, shape, dt, reg="U"), Reg(name)

    U, _ = ub("U", [128, 8, 30 + TOWN]); r_U = [Reg("U%d" % c) for c in range(8)]
    ucs, r_ucs = ub("ucs", [128, 8, NS, 31])
    S.dma("sync", ucs[:, :, :, 0:30], scT_d, writes=[r_ucs])
    GA, r_GA = ub("GA", [128, 32 + TOWN + NS])
    SG, r_SG = ub("SG", [128, 32 + TOWN + NS])
    for c in range(8):
        for part in ("glua", "glub"):
            ci = plan[(part, c)]
            gemmF(ci, xnT, r_xnT, [(0, 512, pg[0][:, :], r_pg[0]), (512, 512, pg[1][:, :], r_pg[1]),
                                   (1024, NS, pgs[:, 0:NS], r_pgs)])
            gemmF(ci, xhist, r_xhist, [(0, 32, pgs[:, 32:64], r_pgs)])
            wdone(ci)
            dstb, rdst = (GA, r_GA) if part == "glua" else (SG, r_SG)
            fn = AF.Copy if part == "glua" else AF.Sigmoid
            for (src, preg, c0, n) in ((pgs[:, 32:64], r_pgs, 0, 32), (pg[0][:, :], r_pg[0], 32, 512),
                                       (pg[1][:, :], r_pg[1], 544, 512), (pgs[:, 0:NS], r_pgs, 1056, NS)):
                S.op("scalar", lambda e, src=src, c0=c0, n=n, dstb=dstb, fn=fn: e.activation(out=dstb[:, c0:c0 + n], in_=src, func=fn),
                     reads=[preg], writes=[rdst])
        S.op("vector", lambda e, c=c: e.tensor_mul(out=U[:, c, :], in0=GA[:, 2:32 + TOWN], in1=SG[:, 2:32 + TOWN]),
             reads=[r_GA, r_SG], writes=[r_U[c]])
        S.op("vector", lambda e, c=c: e.tensor_mul(out=ucs[:, c, :, 30], in0=GA[:, 1056:1056 + NS], in1=SG[:, 1056:1056 + NS]),
             reads=[r_GA, r_SG], writes=[r_ucs])
    S.dma("sync", ncp_d, U[:, :, TOWN:TOWN + 30], reads=r_U, writes=[out_regs["ncp"]])
    S.dma("sync", ncs_d, ucs[:, :, :, 1:31], reads=[r_ucs], writes=[out_regs["ncs"]])


    if stage >= 3:
        u_addr = base + PSZ
        Cb, _ = ub("C", [128, 8, XW]); r_C = [Reg("C%d" % c) for c in range(8)]
        MEAN, r_MEAN = ub("MEAN", [128, 512]); RSTD, r_RSTD = ub("RSTD", [128, 512]); SQ, r_SQ = ub("SQ", [128, 512])
        stm, r_stm = ub("stm", [128, NS, 31])
        for c in range(8):
            cw = lambda k, c=c: cvec[:, CV_CW + c * 31 + k:CV_CW + c * 31 + k + 1]
            S.op("vector", lambda e, c=c, cw=cw: e.tensor_scalar(out=Cb[:, c, 0:TOWN], in0=U[:, c, 0:TOWN], scalar1=cw(0),
                                                                scalar2=cvec[:, CV_CB + c:CV_CB + c + 1], op0=ALU.mult, op1=ALU.add),
                 reads=[r_U[c], r_cvec], writes=[r_C[c]])
            for k in range(1, 31):
                S.op("vector", lambda e, c=c, k=k, cw=cw: e.scalar_tensor_tensor(out=Cb[:, c, 0:TOWN], in0=U[:, c, k:k + TOWN], scalar=cw(k),
                                                                              in1=Cb[:, c, 0:TOWN], op0=ALU.mult, op1=ALU.add),
                     reads=[r_U[c], r_cvec, r_C[c]], writes=[r_C[c]])
            S.op("vector", lambda e, c=c: e.tensor_mul(out=stm[:], in0=ucs[:, c, :, :],
                                                       in1=cvec[:, CV_CW + c * 31:CV_CW + (c + 1) * 31].unsqueeze(1).to_broadcast([128, NS, 31])),
                 reads=[r_ucs, r_cvec], writes=[r_stm])
            S.op("vector", lambda e, c=c: e.tensor_reduce(out=Cb[:, c, TOWN:XW], in_=stm[:], axis=AX.X, op=ALU.add),
                 reads=[r_stm], writes=[r_C[c]])
            S.op("vector", lambda e, c=c: e.tensor_scalar(out=Cb[:, c, TOWN:XW], in0=Cb[:, c, TOWN:XW], scalar1=cvec[:, CV_CB + c:CV_CB + c + 1],
                                                          scalar2=None, op0=ALU.add), reads=[r_C[c], r_cvec], writes=[r_C[c]])
        cbT = nc.alloc_sbuf_tensor_at("cbT_alias", [128, 8, XW], BF16, offset=u_addr)
        r_cb = [Reg("cb%d" % c) for c in range(8)]
        tiles3 = [(0, 512), (512, 512), (1024, NS)]
        for (c0, n) in tiles3:
            for c in range(8):
                S.op("tensor", lambda e, c=c, c0=c0, n=n: e.matmul(pq[0][:, 0:n], lhsT=ONES, rhs=Cb[:, c, c0:c0 + n], start=(c == 0), stop=(c == 7)),
                     reads=[r_msk, r_C[c]], writes=[r_pq[0]], sync_same=False)
            for c in range(8):
                S.op("scalar", lambda e, c=c, c0=c0, n=n: e.activation(out=SQ[:, 0:n], in_=Cb[:, c, c0:c0 + n], func=AF.Square),
                     reads=[r_C[c]], writes=[r_SQ])
                S.op("tensor", lambda e, c=c, n=n: e.matmul(pq[1][:, 0:n], lhsT=ONES, rhs=SQ[:, 0:n], start=(c == 0), stop=(c == 7)),
                     reads=[r_msk, r_SQ], writes=[r_pq[1]], sync_same=False)
            S.op("vector", lambda e, n=n: e.tensor_scalar(out=MEAN[:, 0:n], in0=pq[0][:, 0:n], scalar1=1.0 / DA, scalar2=None, op0=ALU.mult),
                 reads=[r_pq[0]], writes=[r_MEAN])
            S.op("vector", lambda e, n=n: e.tensor_mul(out=SQ[:, 0:n], in0=MEAN[:, 0:n], in1=MEAN[:, 0:n]), reads=[r_MEAN], writes=[r_SQ])
            S.op("vector", lambda e, n=n: e.scalar_tensor_tensor(out=RSTD[:, 0:n], in0=pq[1][:, 0:n], scalar=1.0 / DA, in1=SQ[:, 0:n],
                                                                 op0=ALU.mult, op1=ALU.subtract), reads=[r_pq[1], r_SQ], writes=[r_RSTD])
            S.op("scalar", lambda e, n=n: e.activation(out=RSTD[:, 0:n], in_=RSTD[:, 0:n], func=AF.Sqrt, bias=1e-5), reads=[r_RSTD], writes=[r_RSTD])
            S.op("vector", lambda e, n=n: e.reciprocal(out=RSTD[:, 0:n], in_=RSTD[:, 0:n]), reads=[r_RSTD], writes=[r_RSTD])
            for c in range(8):
                S.op("vector", lambda e, c=c, c0=c0, n=n: e.tensor_sub(out=Cb[:, c, c0:c0 + n], in0=Cb[:, c, c0:c0 + n], in1=MEAN[:, 0:n]),
                     reads=[r_C[c], r_MEAN], writes=[r_C[c]])
                S.op("vector", lambda e, c=c, c0=c0, n=n: e.tensor_mul(out=Cb[:, c, c0:c0 + n], in0=Cb[:, c, c0:c0 + n], in1=RSTD[:, 0:n]),
                     reads=[r_C[c], r_RSTD], writes=[r_C[c]])
                S.op("scalar", lambda e, c=c, c0=c0, n=n: e.activation(out=Cb[:, c, c0:c0 + n], in_=Cb[:, c, c0:c0 + n], func=AF.Silu,
                                                                       scale=cvec[:, CV_CG + c:CV_CG + c + 1], bias=cvec[:, CV_CLB + c:CV_CLB + c + 1]),
                     reads=[r_C[c], r_cvec], writes=[r_C[c]])
        for c in range(8):
            ci = plan[("gb", c)]
            gemmF(ci, xnT, r_xnT, [(0, 512, pg[0][:, :], r_pg[0]), (512, 512, pg[1][:, :], r_pg[1]), (1024, NS, pgs[:, 0:NS], r_pgs)])
            wdone(ci)
            for (c0, n, src, preg) in ((0, 512, pg[0][:, :], r_pg[0]), (512, 512, pg[1][:, :], r_pg[1]), (1024, NS, pgs[:, 0:NS], r_pgs)):
                S.op("scalar", lambda e, n=n, src=src: e.activation(out=SQ[:, 0:n], in_=src, func=AF.Silu), reads=[preg], writes=[r_SQ])
                S.op("vector", lambda e, c=c, c0=c0, n=n: e.tensor_mul(out=cbT[:, c, c0:c0 + n], in0=Cb[:, c, c0:c0 + n], in1=SQ[:, 0:n]),
                     reads=[r_C[c], r_SQ], writes=[r_cb[c]] + r_U + [out_regs["ncp"]])

        reset_union()
        _cb2, _ = ub("cbT2", [128, 8, XW + 7], BF16)
        mT, _ = ub("mT", [128, 16, XW], BF16); r_mT = [Reg("mT%d" % d) for d in range(16)]
        YA, r_YA = ub("YA", [128, XW]); YB, r_YB = ub("YB", [128, XW]); M1, r_M1 = ub("M1", [128, XW]); M2, r_M2 = ub("M2", [128, XW])
        tg = [(0, 512, pg[0][:, :], r_pg[0]), (512, 512, pg[1][:, :], r_pg[1]), (1024, NS, pgs[:, 0:NS], r_pgs)]
        r_ogall = Reg("ogall"); r_cball = Reg("cball")
        for dc in range(16):
            for (key, rhs, rr, dstb, rd, fn) in ((("pa", dc), ogT, r_ogall, YA, r_YA, AF.Copy), (("mga", dc), xnT, r_xnT, M1, r_M1, AF.Sigmoid),
                                                 (("pb", dc), cbT, r_cball, YB, r_YB, AF.Copy), (("mgb", dc), xnT, r_xnT, M2, r_M2, AF.Sigmoid)):
                ci = plan[key]
                gemmF(ci, rhs, rr, tg)
                wdone(ci)
                for (c0, n, src, preg) in tg:
                    S.op("scalar", lambda e, c0=c0, n=n, src=src, dstb=dstb, fn=fn: e.activation(out=dstb[:, c0:c0 + n], in_=src, func=fn),
                         reads=[preg], writes=[rd])
            S.op("vector", lambda e: e.tensor_mul(out=M1[:], in0=M1[:], in1=YA[:]), reads=[r_M1, r_YA], writes=[r_M1])
            S.op("vector", lambda e: e.tensor_mul(out=M2[:], in0=M2[:], in1=YB[:]), reads=[r_M2, r_YB], writes=[r_M2])
            S.op("vector", lambda e, dc=dc: e.tensor_add(out=mT[:, dc, :], in0=M1[:], in1=M2[:]), reads=[r_M1, r_M2], writes=[r_mT[dc]])

        reset_union()
        _keep, _ = ub("keep", [128, (8 * (XW + 7) * 2 + 16 * XW * 2 + 255) // 256 * 64])
        r_mTall = Reg("mTall")
        fgT = ub("fgT", [128, 16])[0]; r_fgT = Reg("fgT")
        S.dma("sync", fgT[:], fgT_d, writes=[r_fgT])
        hT, r_hT = ub("hT", [128, 16, 528])
        xtl, r_xtl = ub("xtl", [128, D])
        TG, r_TG = ub("TG", [128, 528]); RS, r_RS = ub("RS", [128, 528])
        ptl, r_ptl = ub("ptl", [128, 256])
        hTb, r_hTb = xnT, Reg("hTb")
        pT, r_pT = ogT, Reg("pT")
        for hh in range(2):
            t0h = 0 if hh == 0 else 512
            nh = 512 if hh == 0 else 512 + NS
            ntile = (nh + 127) // 128
            for ti in range(ntile):
                n = min(128, nh - ti * 128)
                row = 1024 + t0h + ti * 128
                S.dma("sync", xtl[0:n, :], xs_d[row:row + n, :], writes=[r_xtl])
                S.dma("sync", ptl[0:n, :], pp_d[t0h + ti * 128:t0h + ti * 128 + n, :], writes=[r_ptl])
                for q4 in range(4):
                    for j in range(4):
                        ec = q4 * 4 + j
                        S.op("tensor", lambda e, n=n, ec=ec, j=j: e.transpose(pq[0][:, j * 128:j * 128 + n], xtl[0:n, ec * 128:(ec + 1) * 128], identf[0:n, 0:n]),
                             reads=[r_xtl, r_identf], writes=[r_pq[0]], sync_same=False)
                    S.op("scalar", lambda e, n=n, q4=q4, ti=ti: e.copy(out=hT[:, q4 * 4:(q4 + 1) * 4, ti * 128:ti * 128 + n],
                                                                     in_=pq[0][:, :].rearrange("p (c t) -> p c t", t=128)[:, :, 0:n]),
                         reads=[r_pq[0]], writes=[r_hT])
                for j in range(2):
                    S.op("tensor", lambda e, n=n, j=j: e.transpose(pq[1][:, j * 128:j * 128 + n], ptl[0:n, j * 128:(j + 1) * 128], identf[0:n, 0:n]),
                         reads=[r_ptl, r_identf], writes=[r_pq[1]], sync_same=False)
                S.op("scalar", lambda e, n=n, ti=ti: e.copy(out=pT[:, 0:2, ti * 128:ti * 128 + n],
                                                          in_=pq[1][:, 0:256].rearrange("p (c t) -> p c t", t=128)[:, :, 0:n]),
                     reads=[r_pq[1]], writes=[r_pT])
            tgh = [(t0h, 512, pg[0][:, :], r_pg[0])] + ([(1024, NS, pgs[:, 0:NS], r_pgs)] if hh == 1 else [])
            loc = [(0, 512)] + ([(512, NS)] if hh == 1 else [])
            for ec in range(16):
                ci = plan[("wo", ec, hh)]
                gemmF(ci, mT, r_mTall, tgh)
                wdone(ci)
                for (c0, n, src, preg), (l0, ln) in zip(tgh, loc):
                    S.op("vector", lambda e, ec=ec, src=src, l0=l0, ln=ln: e.tensor_add(out=hT[:, ec, l0:l0 + ln], in0=src, in1=hT[:, ec, l0:l0 + ln]),
                         reads=[preg, r_hT], writes=[r_hT])
            S.op("scalar", lambda e, nh=nh: e.copy(out=hTb[:, :, 0:nh], in_=hT[:, :, 0:nh]), reads=[r_hT], writes=[r_hTb])
            tgl = [(0, 512, pg[0][:, :], r_pg[0])] + ([(512, NS, pgs[:, 0:NS], r_pgs)] if hh == 1 else [])
            tgp = [(0, 512, pg[1][:, :], r_pg[1])] + ([(512, NS, pgs[:, 64:64 + NS], r_pgs)] if hh == 1 else [])
            for fc in range(16):
                ci = plan[("pgt", fc, hh)]
                gemmF(ci, hTb, r_hTb, tgl)
                wdone(ci)
                ci = plan[("ple", fc, hh)]
                gemmF(ci, pT, r_pT, tgp)
                wdone(ci)
                for (c0, n, src, preg), (_, _, srcp, pregp) in zip(tgl, tgp):
                    S.op("scalar", lambda e, c0=c0, n=n, src=src: e.activation(out=TG[:, c0:c0 + n], in_=src, func=AF.Sigmoid), reads=[preg], writes=[r_TG])
                    S.op("scalar", lambda e, c0=c0, n=n, srcp=srcp: e.copy(out=RS[:, c0:c0 + n], in_=srcp), reads=[pregp], writes=[r_RS])
                S.op("vector", lambda e, nh=nh: e.tensor_mul(out=TG[:, 0:nh], in0=TG[:, 0:nh], in1=RS[:, 0:nh]), reads=[r_TG, r_RS], writes=[r_TG])
                S.op("vector", lambda e, fc=fc, nh=nh: e.tensor_add(out=hT[:, fc, 0:nh], in0=hT[:, fc, 0:nh], in1=TG[:, 0:nh]), reads=[r_hT, r_TG], writes=[r_hT])
            for (l0, ln) in loc:
                for fc in range(16):
                    S.op("scalar", lambda e, fc=fc, l0=l0, ln=ln: e.activation(out=TG[:, 0:ln], in_=hT[:, fc, l0:l0 + ln], func=AF.Square),
                         reads=[r_hT], writes=[r_TG])
                    S.op("tensor", lambda e, fc=fc, ln=ln: e.matmul(pq[2][:, 0:ln], lhsT=ONES, rhs=TG[:, 0:ln], start=(fc == 0), stop=(fc == 15)),
                         reads=[r_msk, r_TG], writes=[r_pq[2]], sync_same=False)
                S.op("scalar", lambda e, l0=l0, ln=ln: e.activation(out=RS[:, l0:l0 + ln], in_=pq[2][:, 0:ln], func=AF.Sqrt, scale=1.0 / D, bias=1e-6),
                     reads=[r_pq[2]], writes=[r_RS])
            S.op("vector", lambda e, nh=nh: e.reciprocal(out=RS[:, 0:nh], in_=RS[:, 0:nh]), reads=[r_RS], writes=[r_RS])
            for fc in range(16):
                S.op("vector", lambda e, fc=fc, nh=nh: e.scalar_tensor_tensor(out=hT[:, fc, 0:nh], in0=hT[:, fc, 0:nh], scalar=fgT[:, fc:fc + 1],
                                                                            in1=RS[:, 0:nh], op0=ALU.mult, op1=ALU.mult),
                     reads=[r_hT, r_RS, r_fgT], writes=[r_hT])
            for ti in range(ntile):
                n = min(128, nh - ti * 128)
                for q4 in range(4):
                    for j in range(4):
                        fc = q4 * 4 + j
                        S.op("tensor", lambda e, n=n, fc=fc, j=j, ti=ti: e.transpose(pq[3][0:n, j * 128:(j + 1) * 128], hT[:, fc, ti * 128:ti * 128 + n], identf[:, :]),
                             reads=[r_hT, r_identf], writes=[r_pq[3]], sync_same=False)
                    S.op("scalar", lambda e, n=n, q4=q4: e.copy(out=xtl[0:n, q4 * 512:(q4 + 1) * 512], in_=pq[3][0:n, :]), reads=[r_pq[3]], writes=[r_xtl])
                orow = t0h + ti * 128
                S.dma("sync", y_d[orow:orow + n, :], xtl[0:n, :], reads=[r_xtl], writes=[out_regs["y"]])

    for _i in range(int(os.environ.get("KDUMMY", "0"))):
        S.op("tensor", lambda e: e.matmul(pq[3][:, 0:128], lhsT=identf[:, :], rhs=identf[:, :], start=True, stop=True),
             reads=[r_identf], writes=[r_pq[3]], sync_same=False)
    deps = {}
    for r in out_regs.values():
        if r.w is not None:
            deps[r.w[0]] = max(deps.get(r.w[0], 0), r.w[1])
    S._emit_waits("sync", deps)
    S.emit()
    return nc, S


_CACHE = {}


def _host_inputs(inp):
    f32 = np.float32
    x_prompt = np.asarray(inp["x_prompt"], f32)
    x_sample = np.asarray(inp["x_sample"], f32)[:, 0, :]
    p_prompt = np.asarray(inp["p_prompt"], f32)[0]
    p_sample = np.asarray(inp["p_sample"], f32)[0][:, 0, :]
    state_shift = np.asarray(inp["state_shift"], f32)[0]
    state_wkv = np.asarray(inp["state_wkv"], f32)[0]
    state_conv = np.asarray(inp["state_conv"], f32)[0]

    def fm(v, n):
        return np.ascontiguousarray(np.asarray(v, f32).reshape(n, 128).T)

    cvec = np.zeros((128, CV_N), f32)
    cvec[:, CV_MU:CV_MU + 25] = fm(inp["shift_mu"][0], 25)
    for off, key in ((CV_W0, "w0"), (CV_A0, "a0"), (CV_KK, "k_k"), (CV_KA, "k_a"), (CV_LG, "lnx_g"), (CV_LB, "lnx_b"),
                     (CV_CB, "conv_b"), (CV_CG, "cln_g"), (CV_CLB, "cln_b")):
        cvec[:, off:off + 8] = fm(inp[key][0], 8)
    cvec[:, CV_RK:CV_RK + 8] = fm(np.asarray(inp["r_k"], f32)[0].reshape(-1), 8)
    cw = np.asarray(inp["conv_w"], f32)[0]
    cvec[:, CV_CW:CV_CW + 248] = cw.reshape(31, 8, 128).transpose(2, 1, 0).reshape(128, 248)
    lw = np.ascontiguousarray(np.concatenate([np.asarray(inp["w_lora_b"], f32)[0], np.asarray(inp["a_lora_b"], f32)[0]], 0))
    identb = np.eye(128, dtype=f32).astype(ml_dtypes.bfloat16)
    identf = np.eye(128, dtype=f32)
    msk = np.zeros((128, MK_N), f32)
    sidx = np.arange(128) % 64
    tidx = np.arange(64)
    msk[:, MK_SC:MK_SC + 64] = (tidx[None, :] > sidx[:, None])
    msk[:, MK_SC + 64:MK_SC + 128] = (tidx[None, :] >= sidx[:, None])
    msk[0:64, MK_MU:MK_MU + 64] = (tidx[None, :] > tidx[:, None])
    msk[0:64, MK_ML:MK_ML + 64] = (tidx[None, :] < tidx[:, None])
    msk[:, MK_RM:MK_RM + 512] = (np.arange(512) % 64 != 0)[None, :]
    msk[:, MK_BO:MK_BO + 128] = ((np.arange(128) // 64)[:, None] == (np.arange(128) // 64)[None, :])
    msk[:, MK_ON:MK_ON + 128] = 1.0
    lwp = np.zeros((128, 2, DA), f32)
    lwp[0:64, 0] = np.asarray(inp["w_lora_b"], f32)[0]
    lwp[64:128, 1] = np.asarray(inp["a_lora_b"], f32)[0]
    shared = dict(
        w_in=np.ascontiguousarray(np.asarray(inp["w_in"], f32)[0]),
        w_pa=np.ascontiguousarray(np.asarray(inp["w_proj_a"], f32)[0]),
        w_pb=np.ascontiguousarray(np.asarray(inp["w_proj_b"], f32)[0]),
        w_out=np.ascontiguousarray(np.asarray(inp["w_out"], f32)[0]),
        w_pg=np.ascontiguousarray(np.asarray(inp["w_ple_gate"], f32)[0]),
        w_ple=np.ascontiguousarray(np.asarray(inp["w_ple"], f32)[0]),
        ng=np.ascontiguousarray(np.asarray(inp["norm_g"], f32).reshape(1, D)),
        fg=np.ascontiguousarray(np.asarray(inp["final_g"], f32).reshape(1, D)),
        fgT=np.ascontiguousarray(np.asarray(inp["final_g"], f32).reshape(16, 128).T),
        cvec=cvec, lwp=lwp, identb=identb, identf=identf, msk=msk,
    )
    maps = []
    for c in range(NCORE):
        b, hf = c // 2, c % 2
        own = x_prompt[b, hf * 1024:(hf + 1) * 1024]
        pre = x_prompt[b, 0:1024] if hf == 1 else np.zeros((1024, D), f32)
        sm = slice(c * NS, (c + 1) * NS)
        m = dict(shared)
        m["xs"] = np.ascontiguousarray(np.concatenate([pre, own, x_sample[sm]], 0))
        m["ssT"] = np.ascontiguousarray(state_shift[sm].T.reshape(25, 128, NS).transpose(1, 0, 2))
        m["wkvT"] = np.ascontiguousarray(state_wkv[sm].reshape(NS, 8, 2, 64, 64).transpose(0, 1, 2, 4, 3).reshape(NS, 8, 128, 64))
        m["scT"] = np.ascontiguousarray(state_conv[sm].transpose(2, 0, 1).reshape(8, 128, NS, 30).transpose(1, 0, 2, 3))
        m["pp"] = np.ascontiguousarray(np.concatenate([p_prompt[b, hf * 1024:(hf + 1) * 1024], p_sample[sm]], 0))
        maps.append(m)
    return maps


def _assemble(res):
    f32 = np.float32
    y_prompt = np.zeros((4, 2048, D), f32)
    y_sample = np.zeros((128, 1, D), f32)
    nsp = np.zeros((1, 4, SHIFT_W), f32)
    nwp = np.zeros((1, 4, 16, 64, 64), f32)
    ncp = np.zeros((1, 4, 30, DA), f32)
    nss = np.zeros((1, 128, SHIFT_W), f32)
    nws = np.zeros((1, 128, 16, 64, 64), f32)
    ncs = np.zeros((1, 128, 30, DA), f32)
    for c in range(NCORE):
        r = res[c]
        b, hf = c // 2, c % 2
        sm = slice(c * NS, (c + 1) * NS)
        y_prompt[b, hf * 1024:(hf + 1) * 1024] = r["y"][0:1024]
        y_sample[sm, 0] = r["y"][1024:1024 + NS]
        nss[0, sm] = r["nss"].transpose(1, 0, 2).reshape(SHIFT_W, NS).T
        nws[0, sm] = r["nws"].reshape(NS, 8, 2, 64, 64).transpose(0, 1, 2, 4, 3).reshape(NS, 16, 64, 64)
        ncs[0, sm] = r["ncs"].transpose(1, 0, 2, 3).reshape(DA, NS, 30).transpose(1, 2, 0)
        if hf == 1:
            nsp[0, b] = r["nsp"].T.reshape(SHIFT_W)
            nwp[0, b] = r["nwp"].reshape(2, 64, 8, 64).transpose(2, 0, 3, 1).reshape(16, 64, 64)
            ncp[0, b] = r["ncp"].transpose(1, 0, 2).reshape(DA, 30).T
    return (y_prompt, y_sample, nsp, nwp, ncp, nss, nws, ncs)


def kernel(**inputs):
    maps = _host_inputs(inputs)
    if "nc" not in _CACHE:
        _CACHE["nc"] = build(int(os.environ.get("KSTAGE", "99")))[0]
    nc = _CACHE["nc"]
    res = run_bass_kernel_spmd(nc, maps, core_ids=list(range(NCORE)))
    return _assemble(res.results)
```
